# Optimizing a Trainium2 kernel written in Bass

```python
import math
import jax, jax.numpy as jnp
from jax import lax
import numpy as np

D_MODEL = 1024
BATCH = 4
SEQ = 8192
DEPTH = 2

CHUNK = 64
N_MIXERS = 2
N_ATTN_LAYERS = (DEPTH + 1) // 2
N_CONV_LAYERS = DEPTH // 2
N_HEADS = 8
HEAD_DIM = 64
V_DIM = 2 * HEAD_DIM
ROPE_THETA = 10000.0
Q_BLOCK = 128
CONV_WIDTH = 31
D_FF = -(-(8 * D_MODEL) // (3 * 256)) * 256
LN_EPS = 1e-5
DEEPNORM_ALPHA = (2.0 * DEPTH) ** 0.25
DEEPNORM_BETA = (8.0 * DEPTH) ** -0.25
MASK_VALUE = -1e30

kernel_name = "hybrid_diffattn_conformer_conv_deepnorm"


def layer_norm(x, g, b):
    xf = x.astype(jnp.float32)
    mu = jnp.mean(xf, axis=-1, keepdims=True)
    var = jnp.mean(jnp.square(xf - mu), axis=-1, keepdims=True)
    y = (xf - mu) * lax.rsqrt(var + LN_EPS)
    return (y * g.astype(jnp.float32) + b.astype(jnp.float32)).astype(x.dtype)


def rms_norm(x, g):
    xf = x.astype(jnp.float32)
    y = xf * lax.rsqrt(jnp.mean(jnp.square(xf), axis=-1, keepdims=True) + LN_EPS)
    return (y * g.astype(jnp.float32)).astype(x.dtype)


def rope_tables(seq_len, dim):
    pos = jnp.arange(seq_len, dtype=jnp.float32)
    inv_freq = ROPE_THETA ** (-jnp.arange(0, dim, 2, dtype=jnp.float32) / dim)
    ang = pos[:, None] * inv_freq[None, :]
    return jnp.cos(ang), jnp.sin(ang)


def apply_rope(x, cos, sin):
    half = x.shape[-1] // 2
    x1, x2 = x[..., :half], x[..., half:]
    c = cos[None, :, None, None, :].astype(x.dtype)
    s = sin[None, :, None, None, :].astype(x.dtype)
    return jnp.concatenate([x1 * c - x2 * s, x2 * c + x1 * s], axis=-1)


def diff_attention(x, w_qkv, w_o, lq1, lk1, lq2, lk2, subln_g, lambda_init):
    B, S, D = x.shape
    n_blocks = S // Q_BLOCK
    qkv = x @ w_qkv
    q, k, v = jnp.split(qkv, 3, axis=-1)
    q = q.reshape(B, S, N_HEADS, 2, HEAD_DIM)
    k = k.reshape(B, S, N_HEADS, 2, HEAD_DIM)
    v = v.reshape(B, S, N_HEADS, V_DIM).transpose(0, 2, 1, 3)
    cos, sin = rope_tables(S, HEAD_DIM)
    q = apply_rope(q, cos, sin) * (HEAD_DIM ** -0.5)
    k = apply_rope(k, cos, sin)
    q = q.transpose(0, 2, 3, 1, 4)
    k = k.transpose(0, 2, 3, 1, 4)
    q_blocks = jnp.moveaxis(q.reshape(B, N_HEADS, 2, n_blocks, Q_BLOCK, HEAD_DIM), 3, 0)

    lam = (jnp.exp(jnp.sum(lq1.astype(jnp.float32) * lk1.astype(jnp.float32)))
           - jnp.exp(jnp.sum(lq2.astype(jnp.float32) * lk2.astype(jnp.float32)))
           + lambda_init)
    key_chunk = jnp.arange(S) // CHUNK

    def block_fn(args):
        qb, blk = args
        s = jnp.einsum('bhmqd,bhmkd->bhmqk', qb, k).astype(jnp.float32)
        q_chunk = (blk * Q_BLOCK + jnp.arange(Q_BLOCK)) // CHUNK
        mask = key_chunk[None, :] <= q_chunk[:, None]
        s = jnp.where(mask[None, None, None], s, MASK_VALUE)
        p = jax.nn.softmax(s, axis=-1)
        attn = p[:, :, 0] - lam * p[:, :, 1]
        return jnp.einsum('bhqk,bhkv->bhqv', attn.astype(v.dtype), v)

    out = lax.map(block_fn, (q_blocks, jnp.arange(n_blocks)))
    out = out.transpose(1, 0, 3, 2, 4).reshape(B, S, N_HEADS, V_DIM)
    out = rms_norm(out, subln_g) * (1.0 - lambda_init)
    return out.reshape(B, S, N_HEADS * V_DIM) @ w_o


def conformer_conv(x, w_pw1, b_pw1, w_dw, b_dw, ln_g, ln_b, w_pw2, b_pw2):
    D = x.shape[-1]
    h = x @ w_pw1 + b_pw1
    a, gate = jnp.split(h, 2, axis=-1)
    h = a * jax.nn.sigmoid(gate)
    h = jnp.pad(h, ((0, 0), (CONV_WIDTH - 1, 0), (0, 0)))
    h = lax.conv_general_dilated(
        h, w_dw[:, None, :].astype(h.dtype), window_strides=(1,), padding='VALID',
        dimension_numbers=('NWC', 'WIO', 'NWC'), feature_group_count=D) + b_dw
    h = layer_norm(h, ln_g, ln_b)
    h = jax.nn.silu(h)
    return h @ w_pw2 + b_pw2


def swiglu_ffn(x, w_gate, w_up, w_down):
    return (jax.nn.silu(x @ w_gate) * (x @ w_up)) @ w_down


def setup_inputs(seed: int = 0) -> dict:
    key = jax.random.key(seed)
    ks = jax.random.split(key, 24)
    D = D_MODEL
    nrm = lambda k, shape, scale: jax.random.normal(k, shape, jnp.float32) * scale
    return {
        "x": nrm(ks[0], (BATCH, SEQ, D), 1.0),
        "attn_w_qkv": nrm(ks[1], (N_ATTN_LAYERS, D, 3 * D), D ** -0.5),
        "attn_w_o": nrm(ks[2], (N_ATTN_LAYERS, N_HEADS * V_DIM, D), DEEPNORM_BETA * D ** -0.5),
        "attn_lambda_q1": nrm(ks[3], (N_ATTN_LAYERS, HEAD_DIM), 0.1),
        "attn_lambda_k1": nrm(ks[4], (N_ATTN_LAYERS, HEAD_DIM), 0.1),
        "attn_lambda_q2": nrm(ks[5], (N_ATTN_LAYERS, HEAD_DIM), 0.1),
        "attn_lambda_k2": nrm(ks[6], (N_ATTN_LAYERS, HEAD_DIM), 0.1),
        "attn_subln_g": 1.0 + nrm(ks[7], (N_ATTN_LAYERS, V_DIM), 0.02),
        "conv_w_pw1": nrm(ks[8], (N_CONV_LAYERS, D, 2 * D), D ** -0.5),
        "conv_b_pw1": nrm(ks[9], (N_CONV_LAYERS, 2 * D), 0.02),
        "conv_w_dw": nrm(ks[10], (N_CONV_LAYERS, CONV_WIDTH, D), CONV_WIDTH ** -0.5),
        "conv_b_dw": nrm(ks[11], (N_CONV_LAYERS, D), 0.02),
        "conv_ln_g": 1.0 + nrm(ks[12], (N_CONV_LAYERS, D), 0.02),
        "conv_ln_b": nrm(ks[13], (N_CONV_LAYERS, D), 0.02),
        "conv_w_pw2": nrm(ks[14], (N_CONV_LAYERS, D, D), DEEPNORM_BETA * D ** -0.5),
        "conv_b_pw2": nrm(ks[15], (N_CONV_LAYERS, D), 0.02),
        "ffn_w_gate": nrm(ks[16], (DEPTH, D, D_FF), D ** -0.5),
        "ffn_w_up": nrm(ks[17], (DEPTH, D, D_FF), D ** -0.5),
        "ffn_w_down": nrm(ks[18], (DEPTH, D_FF, D), DEEPNORM_BETA * D_FF ** -0.5),
        "ln_g": 1.0 + nrm(ks[19], (DEPTH, 2, D), 0.02),
        "ln_b": nrm(ks[20], (DEPTH, 2, D), 0.02),
    }


def reference(x, attn_w_qkv, attn_w_o, attn_lambda_q1, attn_lambda_k1, attn_lambda_q2,
              attn_lambda_k2, attn_subln_g, conv_w_pw1, conv_b_pw1, conv_w_dw, conv_b_dw,
              conv_ln_g, conv_ln_b, conv_w_pw2, conv_b_pw2, ffn_w_gate, ffn_w_up,
              ffn_w_down, ln_g, ln_b):
    for i in range(DEPTH):
        j = i // N_MIXERS
        if i % N_MIXERS == 0:
            lambda_init = 0.8 - 0.6 * math.exp(-0.3 * i)
            y = diff_attention(x, attn_w_qkv[j], attn_w_o[j], attn_lambda_q1[j],
                               attn_lambda_k1[j], attn_lambda_q2[j], attn_lambda_k2[j],
                               attn_subln_g[j], lambda_init)
        else:
            y = conformer_conv(x, conv_w_pw1[j], conv_b_pw1[j], conv_w_dw[j], conv_b_dw[j],
                               conv_ln_g[j], conv_ln_b[j], conv_w_pw2[j], conv_b_pw2[j])
        x = layer_norm(DEEPNORM_ALPHA * x + y, ln_g[i, 0], ln_b[i, 0])
        f = swiglu_ffn(x, ffn_w_gate[i], ffn_w_up[i], ffn_w_down[i])
        x = layer_norm(DEEPNORM_ALPHA * x + f, ln_g[i, 1], ln_b[i, 1])
    return x
```

```python
import math
import numpy as np
import ml_dtypes
from contextlib import ExitStack
import concourse.bass as bass
import concourse.mybir as mybir
from concourse.bass_utils import run_bass_kernel_spmd

F32 = mybir.dt.float32
BF16 = mybir.dt.bfloat16
AF = mybir.ActivationFunctionType
ALU = mybir.AluOpType
AX = mybir.AxisListType

D = 1024
SEQ = 8192
NH = 8
DFF = 2816
NF = DFF // 128
CW = 31
EPS = 1e-5
ALPHA = (2.0 * 2) ** 0.25
LAMBDA_INIT0 = 0.8 - 0.6 * math.exp(0.0)
EPOCH = 30000


class Buf:
    __slots__ = ("name", "w", "r")

    def __init__(self, name):
        self.name = name
        self.w = None
        self.r = []


class Op:
    __slots__ = ("eng", "fn", "deps", "needs_inc", "count", "is_dma", "key", "idx", "unit")

    def __init__(self, eng, fn, is_dma=False, key=None, unit=16):
        self.unit = unit
        self.eng = eng
        self.fn = fn
        self.deps = []
        self.needs_inc = False
        self.count = None
        self.is_dma = is_dma
        self.key = key


class Prog:
    ENGS = ("pe", "act", "dve", "pool", "sp")

    def __init__(self, nc):
        self.nc = nc
        self.ops = {e: [] for e in self.ENGS}
        self.dma_keys = {}
        self.dma_units = {}
        self.all_dma = []

    def _add_deps(self, op, reads, writes):
        deps = op.deps
        for b in reads:
            if b.w is not None:
                deps.append(b.w)
        for b in writes:
            if b.w is not None:
                deps.append(b.w)
            deps.extend(b.r)
        for b in reads:
            b.r.append(op)
        for b in writes:
            b.w = op
            b.r = []

    def op(self, eng, fn, reads=(), writes=()):
        o = Op(eng, fn)
        self._add_deps(o, reads, writes)
        self.ops[eng].append(o)
        return o

    def dma(self, queue, out, in_, reads=(), writes=(), key=None):
        o = Op(queue, lambda e: e.dma_start(out=out, in_=in_), is_dma=True, key=key)
        o.needs_inc = True
        self._add_deps(o, reads, writes)
        self.ops[queue].append(o)
        self.all_dma.append(o)
        return o

    def collective(self, fn, reads=(), writes=(), key=None):
        o = Op("pool", fn, is_dma=True, key=key, unit=1)
        o.needs_inc = True
        self._add_deps(o, reads, writes)
        self.ops["pool"].append(o)
        self.all_dma.append(o)
        return o

    def barrier(self):
        lasts = []
        for e in self.ENGS:
            for o in reversed(self.ops[e]):
                if not o.is_dma:
                    lasts.append(o)
                    break
        lastkey = {}
        for o in self.all_dma:
            lastkey[o.key] = o
        lasts.extend(lastkey.values())
        for e in self.ENGS:
            o = Op(e, lambda eng: eng.nop())
            o.deps = list(lasts)
            self.ops[e].append(o)

    def emit(self, es, final_wait=True):
        nc = self.nc
        for e in self.ENGS:
            for o in self.ops[e]:
                for d in o.deps:
                    d.needs_inc = True
        eng_sems = {}
        for e in self.ENGS:
            c = 0
            for o in self.ops[e]:
                if o.is_dma:
                    k = o.key
                    n = self.dma_keys.get(k, 0) + 1
                    self.dma_keys[k] = n
                    o.count = ("dma", k, n * o.unit)
                    self.dma_units[k] = o.unit
                elif o.needs_inc:
                    c += 1
                    o.count = (e, (c - 1) // EPOCH, (c - 1) % EPOCH + 1)
            nep = (c + EPOCH - 1) // EPOCH
            for i in range(max(nep, 1)):
                eng_sems[(e, i)] = es.enter_context(nc.semaphore(f"s_{e}_{i}"))
        dma_sems = {}
        for i, k in enumerate(self.dma_keys):
            dma_sems[k] = es.enter_context(nc.semaphore(f"sd_{i}"))
        self.n_sems = len(eng_sems) + len(dma_sems)

        def semof(cnt):
            if cnt[0] == "dma":
                return dma_sems[cnt[1]], cnt[2], ("dma", cnt[1])
            return eng_sems[(cnt[0], cnt[1])], cnt[2], (cnt[0], cnt[1])

        def run_engine(ename, eng):
            waited = {}
            for o in self.ops[ename]:
                need = {}
                for d in o.deps:
                    if d.eng == ename and not d.is_dma and ename == "pe":
                        continue
                    sem, val, sk = semof(d.count)
                    if waited.get(sk, 0) >= val:
                        continue
                    if need.get(sk, (None, 0))[1] < val:
                        need[sk] = (sem, val)
                items = list(need.items())
                for sk, (sem, val) in items[1:]:
                    eng.wait_ge(sem, val)
                    waited[sk] = val
                ins = o.fn(eng)
                if items:
                    sk, (sem, val) = items[0]
                    ins._wait_ge(sem, val)
                    waited[sk] = val
                if o.needs_inc:
                    sem, val, sk = semof(o.count)
                    if o.is_dma and o.unit == 1:
                        ins.then_inc(sem)
                    else:
                        ins.then_inc(sem, 16 if o.is_dma else 1)
            if ename == "sp" and final_wait:
                for k, n in self.dma_keys.items():
                    eng.wait_ge(dma_sems[k], n * self.dma_units[k])

        block = es.enter_context(nc.Block())

        @block.tensor
        def _(e):
            run_engine("pe", e)

        @block.scalar
        def _(e):
            run_engine("act", e)

        @block.vector
        def _(e):
            run_engine("dve", e)

        @block.gpsimd
        def _(e):
            run_engine("pool", e)

        @block.sync
        def _(e):
            run_engine("sp", e)


class Tile:
    def __init__(self, p, es, name, shape, dtype, space="sbuf", nbufs=1):
        nc = p.nc
        if space == "sbuf":
            self.t = es.enter_context(nc.sbuf_tensor(name, list(shape), dtype))
        else:
            self.t = es.enter_context(nc.psum_tensor(name, list(shape), dtype))
        self.b = [Buf(f"{name}.{i}") for i in range(nbufs)]
        self.shape = shape

    def __getitem__(self, idx):
        return self.t[idx]


def build_attention(nc, es, p, S, io, HPC=4, out_fn=None, block_hook=None):
    NB = S // 512
    NKT = S // 128
    PASS_H = 2
    NPASS = HPC // PASS_H
    xT = io["xT"].rearrange("(kc p) t -> p kc t", p=128)
    T = lambda name, shape, dt, space="sbuf", nbufs=1: Tile(p, es, name, shape, dt, space, nbufs)

    ident = None
    ones_bf = T("a_ones", [128, 128], BF16)
    onesm_bf = T("a_onesm", [128, 128], BF16)
    wq = T("a_wq", [128, 8, PASS_H * 128], BF16)
    wk = T("a_wk", [128, 8, PASS_H * 128], BF16)
    wv = T("a_wv", [128, 8, PASS_H * 128], BF16)
    wst = T("a_wst", [128, 8, PASS_H * 128], F32, nbufs=1)
    KT = T("a_KT", [128, PASS_H, S], BF16, nbufs=PASS_H * NB)
    V = T("a_V", [128, NKT, PASS_H * 128], BF16, nbufs=NB)
    xf = T("a_xf", [128, 2, 8, 512], F32, nbufs=2)
    xb = T("a_xb", [128, 2, 8, 512], BF16, nbufs=2)
    tabc = T("a_tabc", [128, 2, 512], F32, nbufs=2)
    tabs = T("a_tabs", [128, 2, 512], F32, nbufs=2)
    QT = T("a_QT", [128, 2, PASS_H, 512], BF16, nbufs=2 * PASS_H)
    rt1 = T("a_rt1", [128, 2, 512], F32, nbufs=2)
    rt2 = T("a_rt2", [128, 2, 512], F32, nbufs=2)
    NPT = 4
    pt = T("a_pt", [128, NPT, 2, 512], BF16, nbufs=NPT)
    NPP = 4
    ppair = T("a_ppair", [128, NPP, 2, 512], BF16, nbufs=NPP)
    lamt = T("a_lam", [128, 256], F32)
    lamp = T("a_lamp", [128, 128], F32)
    lams = T("a_lams", [128, 8], F32)
    subg = T("a_subg", [128, 2], F32)
    fr = T("a_fr", [128, 3, 512], F32, nbufs=3)
    fo = T("a_fo", [128, 2, 512], F32, nbufs=2)
    fsq = T("a_fsq", [128, 512], BF16)
    frs = T("a_frs", [128, 512], F32)
    fout = T("a_fout", [128, 2, 512], BF16, nbufs=2)
    ps_s = T("a_ps_s", [128, 2, 2, 512], F32, "psum", nbufs=2)
    ps_o = T("a_ps_o", [128, 2, 512], F32, "psum", nbufs=2)
    ps_l = T("a_ps_l", [128, 512], F32, "psum", nbufs=1)
    ps_p = T("a_ps_p", [128, 512], F32, "psum", nbufs=1)

    zmask = T("a_zmask", [128, 2, 64], BF16)
    p.op("pool", lambda e: e.memset(zmask[:], 0.0), writes=[zmask.b[0]])
    p.op("pool", lambda e: e.memset(ones_bf[:], 1.0), writes=[ones_bf.b[0]])
    p.op("pool", lambda e: e.memset(onesm_bf[:], 1.0 / 128.0), writes=[onesm_bf.b[0]])
    p.dma("sp", lamt[:], io["lam"][:, :], writes=[lamt.b[0]], key="a_misc")
    p.op("dve", lambda e: e.tensor_tensor(out=lamp[:, 0:64], in0=lamt[:, 0:64], in1=lamt[:, 64:128], op=ALU.mult),
         reads=[lamt.b[0]], writes=[lamp.b[0]])
    p.op("dve", lambda e: e.tensor_tensor(out=lamp[:, 64:128], in0=lamt[:, 128:192], in1=lamt[:, 192:256], op=ALU.mult),
         reads=[lamt.b[0]], writes=[lamp.b[0]])
    lb = Buf("lams")
    p.op("dve", lambda e: e.reduce_sum(out=lams[:, 0:1], in_=lamp[:, 0:64], axis=AX.X), reads=[lamp.b[0]], writes=[lb])
    p.op("dve", lambda e: e.reduce_sum(out=lams[:, 1:2], in_=lamp[:, 64:128], axis=AX.X), reads=[lamp.b[0]], writes=[lb])
    p.op("act", lambda e: e.activation(out=lams[:, 2:4], in_=lams[:, 0:2], func=AF.Exp), reads=[lb], writes=[lb])
    p.op("dve", lambda e: e.tensor_tensor(out=lams[:, 4:5], in0=lams[:, 3:4], in1=lams[:, 2:3], op=ALU.subtract),
         reads=[lb], writes=[lb])
    p.op("dve", lambda e: e.tensor_scalar(out=lams[:, 5:6], in0=lams[:, 4:5], scalar1=-LAMBDA_INIT0, scalar2=None,
                                          op0=ALU.add), reads=[lb], writes=[lb])
    neg_lam = lams[:, 5:6]
    p.dma("sp", subg[:, 0:1], io["subg"][:, :], writes=[subg.b[0]], key="a_misc2")
    p.op("dve", lambda e: e.tensor_scalar(out=subg[:, 1:2], in0=subg[:, 0:1], scalar1=1.0 - LAMBDA_INIT0, scalar2=None,
                                          op0=ALU.mult), reads=[subg.b[0]], writes=[subg.b[0]])
    gs = subg[:, 1:2]

    def rope_evac(src_ap, src_buf, sl, dst_ap, dst_buf, W=512):
        a1 = rt1[:, sl, :W]
        a2 = rt2[:, sl, :W]
        cc = tabc[:, sl, :W]
        p.op("dve", lambda e: e.tensor_tensor(out=a1, in0=src_ap, in1=cc, op=ALU.mult),
             reads=[src_buf, tabc.b[sl]], writes=[rt1.b[sl]])
        for (o0, i0) in ((0, 32), (32, 0), (64, 96), (96, 64)):
            p.op("dve", (lambda o0, i0: lambda e: e.tensor_tensor(
                out=rt2[o0:o0 + 32, sl, :W], in0=src_ap[i0:i0 + 32, :], in1=tabs[o0:o0 + 32, sl, :W], op=ALU.mult))(o0, i0),
                reads=[src_buf, tabs.b[sl]], writes=[rt2.b[sl]])
        p.op("dve", lambda e: e.tensor_tensor(out=dst_ap, in0=a1, in1=a2, op=ALU.add),
             reads=[rt1.b[sl], rt2.b[sl]], writes=[dst_buf])

    ao = io.get("aoT")
    for ps in range(NPASS):
        for wi, (wt, wsrc) in enumerate(((wq, io["wq"]), (wk, io["wk"]), (wv, io["wv"]))):
            src = wsrc.rearrange("(kc p) n -> p kc n", p=128)[:, :, ps * PASS_H * 128:(ps + 1) * PASS_H * 128]
            p.dma("sp", wst[:], src, writes=[wst.b[0]], key="a_wst")
            p.op("dve", (lambda wt: lambda e: e.tensor_copy(out=wt[:], in_=wst[:]))(wt), reads=[wst.b[0]], writes=[wt.b[0]])

        def load_block(g):
            sl = g % 2
            p.dma("sp", xf[:, sl], xT[:, :, g * 512:(g + 1) * 512], writes=[xf.b[sl]], key=("a_xf", sl))
            p.dma("sp", tabc[:, sl], io["rc"][:, g * 512:(g + 1) * 512], writes=[tabc.b[sl]], key=("a_tc", sl))
            p.dma("sp", tabs[:, sl], io["rs"][:, g * 512:(g + 1) * 512], writes=[tabs.b[sl]], key=("a_ts", sl))
            p.op("dve", lambda e: e.tensor_copy(out=xb[:, sl, 0:4], in_=xf[:, sl, 0:4]), reads=[xf.b[sl]], writes=[xb.b[sl]])
            p.op("dve", lambda e: e.tensor_copy(out=xb[:, sl, 4:8], in_=xf[:, sl, 4:8]), reads=[xf.b[sl]], writes=[xb.b[sl]])

        def project(g):
            sl = g % 2
            for hh in range(PASS_H):
                for kind in ("k", "q"):
                    wt = wk if kind == "k" else wq
                    for kc in range(8):
                        p.op("pe", (lambda kc, wt, hh: lambda e: e.matmul(
                            ps_p[:, :], lhsT=wt[:, kc, hh * 128:(hh + 1) * 128], rhs=xb[:, sl, kc, :],
                            start=(kc == 0), stop=(kc == 7)))(kc, wt, hh),
                            reads=[wt.b[0], xb.b[sl]], writes=[ps_p.b[0]])
                    if kind == "k":
                        rope_evac(ps_p[:, :], ps_p.b[0], sl, KT[:, hh, g * 512:(g + 1) * 512], KT.b[hh * NB + g])
                    else:
                        rope_evac(ps_p[:, :], ps_p.b[0], sl, QT[:, sl, hh, :], QT.b[sl * PASS_H + hh])
                    yield
            for half in range(2):
                for tt in range(2):
                    tok = (half * 2 + tt) * 128
                    for kc in range(8):
                        p.op("pe", (lambda kc, tt, tok: lambda e: e.matmul(
                            ps_p[:, tt * 256:(tt + 1) * 256], lhsT=xb[:, sl, kc, tok:tok + 128], rhs=wv[:, kc, :],
                            start=(kc == 0), stop=(kc == 7)))(kc, tt, tok),
                            reads=[wv.b[0], xb.b[sl]], writes=[ps_p.b[0]])
                kt0 = g * 4 + half * 2
                p.op("dve", (lambda kt0: lambda e: e.tensor_copy(
                    out=V[:, kt0:kt0 + 2, :], in_=ps_p[:, :].rearrange("p (a b) -> p a b", a=2)))(kt0),
                    reads=[ps_p.b[0]], writes=[V.b[g]])
                yield

        state = {"pt": 0, "fin": 0, "sb": 0, "pp": 0}
        LAG = 2
        SUM_LAG = 6

        def run_fin2():
            f2 = state.get("fin2")
            if f2 is not None:
                state["fin2"] = None
                f2()
            if state.get("post") is not None:
                pg = state["post"]
                state["post"] = None
                if block_hook is not None:
                    block_hook(*pg)

        def attend(g, hh, bg, bg_every):
            nkt = 4 * g + 4
            q_ap = QT[:, g % 2, hh]
            q_buf = QT.b[(g % 2) * PASS_H + hh]
            pend = []
            spend = []

            def pv(kt, slot, c0):
                last = (kt == nkt - 1)
                for s in range(2):
                    p.op("pe", (lambda s: lambda e: e.matmul(
                        ps_o[:, s, c0:512], lhsT=V[:, kt, hh * 128:(hh + 1) * 128], rhs=pt[:, slot, s, c0:512],
                        start=(kt == 0), stop=last, skip_group_check=True))(s),
                        reads=[V.b[kt // 4], pt.b[slot]], writes=[ps_o.b[s]])

            def sums(kt, slot, c0, sum_src):
                last = (kt == nkt - 1)
                if sum_src[0] == "pt":
                    src, sbuf_, first = pt[:, slot], pt.b[slot], (kt == 0)
                else:
                    src, sbuf_, first = ppair[:, sum_src[1]], ppair.b[sum_src[1]], (sum_src[2] == 0)
                for s in range(2):
                    p.op("pe", (lambda s: lambda e: e.matmul(
                        ps_l[64 * s:64 * s + 64, c0:512], lhsT=ones_bf[:, 0:64], rhs=src[:, s, c0:512],
                        start=first, stop=last, skip_group_check=True, tile_position=(0, 64 * s)))(s),
                        reads=[ones_bf.b[0], sbuf_], writes=[ps_l.b[0]])

            def score(kt, c0, slot, diag, sb):
                kb = KT.b[hh * NB + kt // 4]
                for s in range(2):
                    p.op("pe", (lambda s: lambda e: e.matmul(
                        ps_s[:, sb, s, c0:512], lhsT=KT[64 * s:64 * s + 64, hh, kt * 128:(kt + 1) * 128],
                        rhs=q_ap[64 * s:64 * s + 64, c0:512], start=True, stop=True))(s),
                        reads=[kb, q_buf], writes=[ps_s.b[sb]])
                p.op("act", lambda e: e.activation(
                    out=pt[:, slot, :, c0:512], in_=ps_s[:, sb, :, c0:512], func=AF.Exp, scale=0.125),
                    reads=[ps_s.b[sb]], writes=[pt.b[slot]])
                if diag:
                    p.op("act", lambda e: e.activation(out=pt[64:128, slot, :, c0:c0 + 64], in_=zmask[64:128, :, :], func=AF.Identity),
                         reads=[zmask.b[0]], writes=[pt.b[slot]])

            for kt in range(nkt):
                j = kt - 4 * g
                c0 = 128 * j if j > 0 else 0
                slot = state["pt"] % NPT
                state["pt"] += 1
                sb = state["sb"] % 2
                state["sb"] += 1
                score(kt, c0, slot, j >= 0, sb)
                if j < 0 and kt % 2 == 0:
                    sum_src = None
                    prev_slot = slot
                elif j < 0:
                    ps_ = state["pp"] % NPP
                    state["pp"] += 1
                    (lambda a, b_, ps_: p.op("dve", lambda e: e.tensor_tensor(out=ppair[:, ps_], in0=pt[:, a], in1=pt[:, b_], op=ALU.add),
                                             reads=[pt.b[a], pt.b[b_]], writes=[ppair.b[ps_]]))(prev_slot, slot, ps_)
                    sum_src = ("pair", ps_, kt - 1)
                else:
                    sum_src = ("pt", slot)
                if len(pend) >= LAG:
                    pv(*pend.pop(0))
                pend.append((kt, slot, c0))
                while spend and (kt - spend[0][0] >= (SUM_LAG if spend[0][3][0] == "pair" else LAG)):
                    sums(*spend.pop(0))
                if sum_src is not None:
                    spend.append((kt, slot, c0, sum_src))
                if bg is not None and bg_every and (kt % bg_every) == bg_every - 1:
                    next(bg, None)
                if kt == 2:
                    run_fin2()
            while pend:
                pv(*pend.pop(0))
            while spend:
                sums(*spend.pop(0))
            fb = state["fin"] % 2
            state["fin"] += 1
            fb2 = 1 - fb
            p.op("act", lambda e: e.activation(out=fr[:, 0, :], in_=ps_l[:, :], func=AF.Ln), reads=[ps_l.b[0]], writes=[fr.b[0]])
            p.op("dve", lambda e: e.tensor_copy(out=fo[:, fb, :], in_=ps_o[:, 0, :]), reads=[ps_o.b[0]], writes=[fo.b[fb]])
            p.op("dve", lambda e: e.tensor_copy(out=fo[:, fb2, :], in_=ps_o[:, 1, :]), reads=[ps_o.b[1]], writes=[fo.b[fb2]])
            for (o0, i0, dst) in ((0, 0, 1), (64, 0, 1), (0, 64, 2), (64, 64, 2)):
                p.op("act", (lambda o0, i0, dst: lambda e: e.activation(out=fr[o0:o0 + 64, dst, :], in_=fr[i0:i0 + 64, 0, :],
                                                                        func=AF.Exp, scale=-1.0))(o0, i0, dst),
                     reads=[fr.b[0]], writes=[fr.b[dst]])
            p.op("dve", lambda e: e.tensor_tensor(out=fo[:, fb, :], in0=fo[:, fb, :], in1=fr[:, 1, :], op=ALU.mult),
                 reads=[fr.b[1]], writes=[fo.b[fb]])
            p.op("dve", lambda e: e.scalar_tensor_tensor(out=fo[:, fb2, :], in0=fo[:, fb2, :], scalar=neg_lam, in1=fr[:, 2, :],
                                                         op0=ALU.mult, op1=ALU.mult),
                 reads=[fr.b[2], lb], writes=[fo.b[fb2]])
            p.op("dve", lambda e: e.tensor_tensor(out=fo[:, fb, :], in0=fo[:, fb, :], in1=fo[:, fb2, :], op=ALU.add),
                 reads=[fo.b[fb2]], writes=[fo.b[fb]])
            p.op("dve", lambda e: e.tensor_tensor(out=fsq[:], in0=fo[:, fb, :], in1=fo[:, fb, :], op=ALU.mult),
                 reads=[fo.b[fb]], writes=[fsq.b[0]])
            def fin2():
                p.op("pe", lambda e: e.matmul(ps_p[:, :], lhsT=onesm_bf[:], rhs=fsq[:], start=True, stop=True),
                     reads=[onesm_bf.b[0], fsq.b[0]], writes=[ps_p.b[0]])
                p.op("act", lambda e: e.activation(out=frs[:], in_=ps_p[:, :], func=AF.Ln, bias=EPS),
                     reads=[ps_p.b[0]], writes=[frs.b[0]])
                p.op("act", lambda e: e.activation(out=frs[:], in_=frs[:], func=AF.Exp, scale=-0.5),
                     reads=[frs.b[0]], writes=[frs.b[0]])
                p.op("dve", lambda e: e.scalar_tensor_tensor(out=fout[:, fb, :], in0=fo[:, fb, :], scalar=gs, in1=frs[:],
                                                             op0=ALU.mult, op1=ALU.mult),
                     reads=[fo.b[fb], frs.b[0], subg.b[0]], writes=[fout.b[fb]])
                if out_fn is not None:
                    out_fn(ps, hh, g, fout, fb)
                else:
                    hrow = (ps * PASS_H + hh) * 128
                    p.dma("sp", ao[hrow:hrow + 128, g * 512:(g + 1) * 512], fout[:, fb, :], reads=[fout.b[fb]],
                          key=("a_out", fb))
            state["fin2"] = fin2

        load_block(0)
        for _ in project(0):
            pass
        for g in range(NB):
            bg = None
            if g + 1 < NB:
                load_block(g + 1)
                bg = project(g + 1)
            nsteps = (4 * g + 4) * PASS_H
            bg_every = max(1, nsteps // 8)
            for hh in range(PASS_H):
                attend(g, hh, bg, bg_every)
            if bg is not None:
                for _ in bg:
                    pass
            state["post"] = (ps, g)
            if g == NB - 1:
                run_fin2()


def rope_tables_np(S):
    pos = np.arange(S, dtype=np.float32)
    inv_freq = (10000.0 ** (-np.arange(0, 64, 2, dtype=np.float32) / 64)).astype(np.float32)
    ang = pos[None, :] * inv_freq[:, None]
    c = np.cos(ang).astype(np.float32)
    s = np.sin(ang).astype(np.float32)
    rc = np.concatenate([c, c, c, c], axis=0)
    rs = np.concatenate([-s, s, -s, s], axis=0)
    return np.ascontiguousarray(rc), np.ascontiguousarray(rs)


def build_prog_attention(S=SEQ, HPC=4):
    nc = bass.Bass("TRN2", target_bir_lowering=False)
    io = {}
    io["xT"] = nc.dram_tensor("xT", [D, S], F32, kind="ExternalInput").ap()
    for n in ("wq", "wk", "wv"):
        io[n] = nc.dram_tensor(n, [D, HPC * 128], F32, kind="ExternalInput").ap()
    io["rc"] = nc.dram_tensor("rc", [128, S], F32, kind="ExternalInput").ap()
    io["rs"] = nc.dram_tensor("rs", [128, S], F32, kind="ExternalInput").ap()
    io["lam"] = nc.dram_tensor("lam", [128, 256], F32, kind="ExternalInput").ap()
    io["subg"] = nc.dram_tensor("subg", [128, 1], F32, kind="ExternalInput").ap()
    io["aoT"] = nc.dram_tensor("aoT", [HPC * 128, S], BF16, kind="ExternalOutput").ap()
    with ExitStack() as es:
        p = Prog(nc)
        build_attention(nc, es, p, S, io, HPC)
        p.emit(es)
    return nc


def attention_inputs(x_b, w_qkv, lq1, lk1, lq2, lk2, subg, h0, HPC, S):
    rc, rs = rope_tables_np(S)
    lam = np.concatenate([lq1, lk1, lq2, lk2]).astype(np.float32)[None, :].repeat(128, axis=0)
    return {
        "xT": np.ascontiguousarray(x_b[:S].T),
        "wq": np.ascontiguousarray(w_qkv[:, h0 * 128:(h0 + HPC) * 128]),
        "wk": np.ascontiguousarray(w_qkv[:, D + h0 * 128:D + (h0 + HPC) * 128]),
        "wv": np.ascontiguousarray(w_qkv[:, 2 * D + h0 * 128:2 * D + (h0 + HPC) * 128]),
        "rc": rc, "rs": rs, "lam": np.ascontiguousarray(lam),
        "subg": np.ascontiguousarray(subg.reshape(128, 1).astype(np.float32)),
    }


VC_LN = 0
VC_BPW1 = 64
VC_WDW = 80
VC_BDW = 328
VC_CLNG = 336
VC_CLNB = 344
VC_BPW2 = 352
NVEC = 360
RING_ELEMS = 2944
NSLOT = 6


def pack_vecs(inp):
    cols = lambda v: np.asarray(v, np.float32).reshape(-1, 128).T
    out = np.zeros((128, NVEC), np.float32)
    for i in range(2):
        for which in range(2):
            base = VC_LN + ((i * 2 + which) * 2) * 8
            out[:, base:base + 8] = cols(inp["ln_g"][i, which])
            out[:, base + 8:base + 16] = cols(inp["ln_b"][i, which])
    out[:, VC_BPW1:VC_BPW1 + 16] = cols(inp["conv_b_pw1"][0])
    wdw = np.asarray(inp["conv_w_dw"][0], np.float32).reshape(CW, 8, 128).transpose(2, 0, 1).reshape(128, CW * 8)
    out[:, VC_WDW:VC_WDW + CW * 8] = wdw
    out[:, VC_BDW:VC_BDW + 8] = cols(inp["conv_b_dw"][0])
    out[:, VC_CLNG:VC_CLNG + 8] = cols(inp["conv_ln_g"][0])
    out[:, VC_CLNB:VC_CLNB + 8] = cols(inp["conv_ln_b"][0])
    out[:, VC_BPW2:VC_BPW2 + 8] = cols(inp["conv_b_pw2"][0])
    return out


def declare_local_io(nc, NT, with_a=True):
    io = {}
    if with_a:
        io["aT"] = nc.dram_tensor("aT", [D, NT], BF16, kind="ExternalInput").ap()
    io["xTl"] = nc.dram_tensor("xTl", [D, NT], F32, kind="ExternalInput").ap()
    for n, shp in (("wo", [D, D]), ("wg0", [D, DFF]), ("wu0", [D, DFF]), ("wd0", [DFF, D]), ("pw1", [D, 2 * D]),
                   ("pw2", [D, D]), ("wg1", [D, DFF]), ("wu1", [D, DFF]), ("wd1", [DFF, D])):
        io[n] = nc.dram_tensor(n, shp, F32, kind="ExternalInput").ap()
    io["vecs"] = nc.dram_tensor("vecs", [128, NVEC], F32, kind="ExternalInput").ap()
    io["ident"] = nc.dram_tensor("ident", [128, 128], F32, kind="ExternalInput").ap()
    io["hscale"] = nc.dram_tensor("hscale", [128, 1], F32, kind="ExternalInput").ap()
    io["oT"] = nc.dram_tensor("oT", [D, NT - 128], F32, kind="ExternalOutput").ap()
    return io


def make_prep(nc, es, p, io, prepq="act"):
    T = lambda name, shape, dt, space="sbuf", nbufs=1: Tile(p, es, name, shape, dt, space, nbufs)
    def scratch(name, nj, per):
        t = nc.dram_tensor(name, [nj, 128, per], BF16, kind="Internal").ap()
        return t, [Buf(f"{name}.{j}") for j in range(nj)]
    s_wo, b_wo = scratch("s_wo", 8, 1024)
    s_gu = [scratch("s_gu0", NF, 2048), scratch("s_gu1", NF, 2048)]
    s_wd = [scratch("s_wd0", 8, NF * 128), scratch("s_wd1", 8, NF * 128)]
    s_pw1, b_pw1 = scratch("s_pw1", 8, 2048)
    s_pw2, b_pw2 = scratch("s_pw2", 8, 1024)

    s_dg, b_dg = scratch("s_dg", 16, 2048)
    stg = T("b_stg", [128, 2, 1408], F32, nbufs=2)
    stb = T("b_stb", [128, 2, 1408], BF16, nbufs=2)
    ident = T("b_ident", [128, 128], F32)
    pvec = T("b_pvec", [128, CW * 8], F32)
    dgs = T("b_dgs", [128, 1, 2048], BF16, nbufs=1)
    p.dma("sp", ident[:], io["ident"][:, :], writes=[ident.b[0]], key="b_ident")
    p.dma("sp", pvec[:], io["vecs"][:, VC_WDW:VC_WDW + CW * 8], writes=[pvec.b[0]], key="b_pvec")
    dst_ = {"i": 0}

    def prep_diag(j, half):
        sl = 0
        k0 = 16 * half
        nk = 16 if half == 0 else CW - 16
        for t in range(nk):
            k = k0 + t
            (lambda t, k: p.op("dve", lambda e: e.tensor_scalar(out=dgs[:, sl, t * 128:(t + 1) * 128], in0=ident[:],
                                                                scalar1=pvec[:, k * 8 + j:k * 8 + j + 1], scalar2=None, op0=ALU.mult),
                               reads=[ident.b[0], pvec.b[0]], writes=[dgs.b[sl]]))(t, k)
        p.dma(prepq, s_dg[2 * j + half, :, :nk * 128], dgs[:, sl, :nk * 128], reads=[dgs.b[sl]], writes=[b_dg[2 * j + half]],
              key=("b_dgs", sl))
    pst = {"i": 0}

    pend = []

    def flush():
        for sl, (src_ap, nel, dst_ap, dst_buf) in enumerate(pend):
            kc = nel // 128
            p.dma(prepq, stg[:, sl, :nel].rearrange("p (k c) -> p k c", k=kc), src_ap, writes=[stg.b[sl]], key=("b_stg", sl))
        for sl, (src_ap, nel, dst_ap, dst_buf) in enumerate(pend):
            (lambda sl, nel: p.op("pool", lambda e: e.tensor_copy(out=stb[:, sl, :nel], in_=stg[:, sl, :nel]),
                                  reads=[stg.b[sl]], writes=[stb.b[sl]]))(sl, nel)
            p.dma(prepq, dst_ap, stb[:, sl, :nel], reads=[stb.b[sl]], writes=[dst_buf], key=("b_stb", sl))
        pend.clear()

    def prep_unit(src_ap, nel, dst_ap, dst_buf):
        pend.append((src_ap, nel, dst_ap, dst_buf))
        if len(pend) == 2:
            flush()

    def prep_k1024(w_ap, col0, dst_scr, j, off, dst_buf):
        src = w_ap.rearrange("(kc p) n -> p kc n", p=128)[:, :, col0:col0 + 128]
        prep_unit(src, 1024, dst_scr[j, :, off:off + 1024], dst_buf)

    def prep_wd(w_ap, dst_scr, j, dst_buf):
        v = w_ap.rearrange("(kc p) n -> p kc n", p=128)
        for half in range(2):
            prep_unit(v[:, half * 11:(half + 1) * 11, j * 128:(j + 1) * 128], 1408,
                      dst_scr[j, :, half * 1408:(half + 1) * 1408], dst_buf)

    def prep_gen():
        for j in range(8):
            prep_k1024(io["wo"], j * 128, s_wo, j, 0, b_wo[j])
            yield
        for j in range(8):
            for half in range(2):
                prep_diag(j, half)
                yield
        for i in range(2):
            if i == 1:
                for j in range(8):
                    prep_k1024(io["pw1"], j * 128, s_pw1, j, 0, b_pw1[j])
                    prep_k1024(io["pw1"], D + j * 128, s_pw1, j, 1024, b_pw1[j])
                    yield
                for j in range(8):
                    prep_k1024(io["pw2"], j * 128, s_pw2, j, 0, b_pw2[j])
                    yield
            for j in range(NF):
                prep_k1024(io[f"wg{i}"], j * 128, s_gu[i][0], j, 0, s_gu[i][1][j])
                prep_k1024(io[f"wu{i}"], j * 128, s_gu[i][0], j, 1024, s_gu[i][1][j])
                yield
            for j in range(8):
                prep_wd(io[f"wd{i}"], s_wd[i][0], j, s_wd[i][1][j])
                yield
        flush()

    scr = dict(s_wo=s_wo, b_wo=b_wo, s_gu=s_gu, s_wd=s_wd, s_pw1=s_pw1, b_pw1=b_pw1, s_pw2=s_pw2, b_pw2=b_pw2,
               s_dg=s_dg, b_dg=b_dg)
    return scr, prep_gen()


def build_local(nc, es, p, io, NT, scr, load_a=None):
    T = lambda name, shape, dt, space="sbuf", nbufs=1: Tile(p, es, name, shape, dt, space, nbufs)
    s_wo, b_wo, s_gu, s_wd = scr["s_wo"], scr["b_wo"], scr["s_gu"], scr["s_wd"]
    s_pw1, b_pw1, s_pw2, b_pw2 = scr["s_pw1"], scr["b_pw1"], scr["s_pw2"], scr["b_pw2"]
    s_dg, b_dg = scr["s_dg"], scr["b_dg"]
    vec = T("b_vec", [128, NVEC], F32)
    hsc = T("b_hsc", [128, 1], F32)
    onesm = T("b_onesm", [128, 128], BF16)
    ring = T("b_ring", [128, NSLOT, RING_ELEMS], BF16, nbufs=NSLOT)
    X0 = T("b_x0", [128, 1, 8, 512], F32, nbufs=8)
    Abf = T("b_abf", [128, 2, 8, 512], BF16, nbufs=8)
    XA = T("b_xa", [128, 8, 512], F32, nbufs=8)
    XB = T("b_xb", [128, 8, 512], F32, nbufs=8)
    XbfA = T("b_xbfa", [128, 8, 512], BF16, nbufs=8)
    XbfB = T("b_xbfb", [128, 8, 512], BF16, nbufs=8)
    XC = T("b_xc", [128, 8, 512], F32, nbufs=8)
    XbfC = T("b_xbfc", [128, 8, 512], BF16, nbufs=8)
    zb = T("b_zb", [128, 3, 512], BF16, nbufs=3)
    zsq = T("b_zsq", [128, 3, 512], BF16, nbufs=3)
    hT = T("b_hT", [128, NF, 512], BF16, nbufs=NF)
    hbuf = T("b_hbuf", [128, 8, 544], BF16, nbufs=8)
    hb_tail = [Buf(f"hb_tail{j}") for j in range(8)]
    tmpa = T("b_tmpa", [128, 2, 512], F32, nbufs=2)
    tn1 = T("b_tn1", [128, 2, 512], F32, nbufs=2)
    tn2 = T("b_tn2", [128, 2, 512], F32, nbufs=2)
    st_msq = T("b_msq", [128, 512], F32)
    st_mean = T("b_mean", [128, 512], F32)
    st_var = T("b_var", [128, 512], F32)
    st_rstd = T("b_rstd", [128, 512], F32)
    pg = T("b_pg", [128, 4, 512], F32, "psum", nbufs=4)
    pp = T("b_pp", [128, 2, 512], F32, "psum", nbufs=2)
    pst_ = T("b_pst", [128, 2, 512], F32, "psum", nbufs=2)

    p.op("pool", lambda e: e.memset(onesm[:], 1.0 / 1024.0), writes=[onesm.b[0]])
    p.dma("sp", vec[:], io["vecs"][:, :], writes=[vec.b[0]], key="b_vec")
    p.dma("sp", hsc[:], io["hscale"][:, :], writes=[hsc.b[0]], key="b_hsc")
    vcol = lambda c: vec[:, c:c + 1]

    rs = {"i": 0, "z": 0, "t": 0, "n": 0, "pp": 0, "pg": 0}

    def fetch(src_ap, nel, src_buf):
        slot = rs["i"] % NSLOT
        rs["i"] += 1
        p.dma("sp", ring[:, slot, :nel], src_ap, reads=[src_buf], writes=[ring.b[slot]], key=("b_ring", slot))
        return slot

    def mm(out, lhsT, rhs, start, stop, reads, writes):
        p.op("pe", lambda e: e.matmul(out, lhsT=lhsT, rhs=rhs, start=start, stop=stop), reads=reads, writes=writes)

    xT3 = io["xTl"].rearrange("(kc p) t -> p kc t", p=128)
    aT3 = io["aT"].rearrange("(h p) t -> p h t", p=128) if "aT" in io else None
    oT3 = io["oT"].rearrange("(kc p) t -> p kc t", p=128)

    def load_group(gi, tok0, W):
        sl = 0
        p.dma("sp", X0[:, sl, :, :W], xT3[:, :, tok0:tok0 + W], writes=[X0.b[sl * 8 + j] for j in range(8)], key=("b_x0", sl))
        if load_a is not None:
            load_a(gi, tok0, W, Abf, hsc)
        else:
            p.dma("sp", Abf[:, sl, :, :W], aT3[:, :, tok0:tok0 + W], writes=Abf.b[0:4], key=("b_abf", sl))

    def ln_core(W, dstx, src_is_dst_bufs, gcol, bcol, out_fn):
        p.op("act", lambda e: e.activation(out=st_msq[:, :W], in_=pst_[:, 0, :W], func=AF.Square),
             reads=[pst_.b[0]], writes=[st_msq.b[0]])
        p.op("act", lambda e: e.activation(out=st_mean[:, :W], in_=pst_[:, 0, :W], func=AF.Identity),
             reads=[pst_.b[0]], writes=[st_mean.b[0]])
        p.op("dve", lambda e: e.tensor_tensor(out=st_var[:, :W], in0=pst_[:, 1, :W], in1=st_msq[:, :W], op=ALU.subtract),
             reads=[pst_.b[1], st_msq.b[0]], writes=[st_var.b[0]])
        p.op("act", lambda e: e.activation(out=st_var[:, :W], in_=st_var[:, :W], func=AF.Ln, bias=EPS),
             reads=[st_var.b[0]], writes=[st_var.b[0]])
        p.op("act", lambda e: e.activation(out=st_rstd[:, :W], in_=st_var[:, :W], func=AF.Exp, scale=-0.5),
             reads=[st_var.b[0]], writes=[st_rstd.b[0]])
        for j in range(8):
            s1 = rs["n"] % 2
            rs["n"] += 1
            (lambda j, s1: (
                p.op("pool" if j % 2 == 0 else "dve",
                     lambda e: e.tensor_tensor(out=tn1[:, s1, :W], in0=dstx[:, j, :W], in1=st_mean[:, :W], op=ALU.subtract),
                     reads=[dstx.b[j], st_mean.b[0]], writes=[tn1.b[s1]]),
                p.op("dve", lambda e: e.tensor_tensor(out=tn2[:, s1, :W], in0=tn1[:, s1, :W], in1=st_rstd[:, :W], op=ALU.mult),
                     reads=[tn1.b[s1], st_rstd.b[0]], writes=[tn2.b[s1]]),
                out_fn(j, tn2[:, s1, :W], tn2.b[s1])))(j, s1)

    def stats_mm(W, j, s):
        mm(pst_[:, 0, :W], onesm[:], zb[:, s, :W], j == 0, j == 7, [onesm.b[0], zb.b[s]], [pst_.b[0]])
        mm(pst_[:, 1, :W], onesm[:], zsq[:, s, :W], j == 0, j == 7, [onesm.b[0], zsq.b[s]], [pst_.b[1]])

    def z_stats(W, dstx, j):
        s = rs["z"] % 3
        rs["z"] += 1
        p.op("dve", lambda e: e.tensor_copy(out=zb[:, s, :W], in_=dstx[:, j, :W]), reads=[dstx.b[j]], writes=[zb.b[s]])
        p.op("act", lambda e: e.activation(out=zsq[:, s, :W], in_=dstx[:, j, :W], func=AF.Square), reads=[dstx.b[j]], writes=[zsq.b[s]])
        return s

    def resid_ln(W, srcx_ap_fn, srcx_buf_fn, dstx, dstbf, lnidx, proj_chunk, bias_col=None, store=None):
        gcol = VC_LN + lnidx * 16
        bcol = gcol + 8
        pend = None
        for j in range(8):
            pa, pb = proj_chunk(j)
            if bias_col is not None:
                t = rs["t"] % 2
                rs["t"] += 1
                (lambda j, t, pa, pb: p.op("act", lambda e: e.activation(out=tmpa[:, t, :W], in_=pa, func=AF.Identity,
                                                                         bias=vcol(bias_col + j)),
                                           reads=[pb, vec.b[0]], writes=[tmpa.b[t]]))(j, t, pa, pb)
                pa, pb = tmpa[:, t, :W], tmpa.b[t]
            (lambda j, pa, pb: p.op("dve", lambda e: e.scalar_tensor_tensor(
                out=dstx[:, j, :W], in0=srcx_ap_fn(j), scalar=ALPHA, in1=pa, op0=ALU.mult, op1=ALU.add),
                reads=[srcx_buf_fn(j), pb], writes=[dstx.b[j]]))(j, pa, pb)
            s = z_stats(W, dstx, j)
            if pend is not None:
                stats_mm(W, *pend)
            pend = (j, s)
        stats_mm(W, *pend)

        def out_fn(j, t2, t2b):
            if dstbf is not None:
                p.op("act", lambda e: e.activation(out=dstbf[:, j, :W], in_=t2, func=AF.Identity, scale=vcol(gcol + j), bias=vcol(bcol + j)),
                     reads=[t2b, vec.b[0]], writes=[dstbf.b[j]])
            p.op("act", lambda e: e.activation(out=dstx[:, j, :W], in_=t2, func=AF.Identity, scale=vcol(gcol + j), bias=vcol(bcol + j)),
                 reads=[t2b, vec.b[0]], writes=[dstx.b[j]])
            if store is not None:
                store(j)
        ln_core(W, dstx, None, gcol, bcol, out_fn)

    def next_pp():
        b = rs["pp"] % 2
        rs["pp"] += 1
        return b

    def ffn(W, i, xbf, srcx, dstx, dstbf, lnidx, store=None, mid_hook=None):
        s_g, b_g = s_gu[i]
        s_d, b_d = s_wd[i]

        def ffn_epi(j, pb0):
            t_ = rs["t"] % 2
            rs["t"] += 1
            p.op("act", lambda e: e.activation(out=tmpa[:, t_, :W], in_=pg[:, pb0, :W], func=AF.Silu),
                 reads=[pg.b[pb0]], writes=[tmpa.b[t_]])
            p.op("dve", lambda e: e.tensor_tensor(out=hT[:, j, :W], in0=tmpa[:, t_, :W], in1=pg[:, pb0 + 1, :W], op=ALU.mult),
                 reads=[tmpa.b[t_], pg.b[pb0 + 1]], writes=[hT.b[j]])

        j = 0
        while j < NF:
            nj = 2 if j == 0 else 1
            slots, pbs = [], []
            for jj in range(nj):
                slots.append(fetch(s_g[j + jj, :, :], 2048, b_g[j + jj]))
                pbs.append((rs["pg"] % 2) * 2)
                rs["pg"] += 1
            if nj == 2:
                for kc in range(8):
                    for jj in range(2):
                        for t in range(2):
                            off = t * 1024 + kc * 128
                            mm(pg[:, pbs[jj] + t, :W], ring[:, slots[jj], off:off + 128], xbf[:, kc, :W], kc == 0, kc == 7,
                               [ring.b[slots[jj]], xbf.b[kc]], [pg.b[pbs[jj] + t]])
            else:
                for t in range(2):
                    for kc in range(8):
                        off = t * 1024 + kc * 128
                        mm(pg[:, pbs[0] + t, :W], ring[:, slots[0], off:off + 128], xbf[:, kc, :W], kc == 0, kc == 7,
                           [ring.b[slots[0]], xbf.b[kc]], [pg.b[pbs[0] + t]])
            for jj in range(nj):
                ffn_epi(j + jj, pbs[jj])
            j += nj

        if mid_hook is not None:
            mid_hook()

        def proj_chunk(j):
            slot = fetch(s_d[j, :, :], NF * 128, b_d[j])
            b = next_pp()
            for kc in range(NF):
                mm(pp[:, b, :W], ring[:, slot, kc * 128:(kc + 1) * 128], hT[:, kc, :W], kc == 0, kc == NF - 1,
                   [ring.b[slot], hT.b[kc]], [pp.b[b]])
            return pp[:, b, :W], pp.b[b]
        resid_ln(W, lambda j: srcx[:, j, :W], lambda j: srcx.b[j], dstx, dstbf, lnidx, proj_chunk, store=store)

    def stage_wo(W):
        sl = 0

        def proj_wo(j):
            slot = fetch(s_wo[j, :, :], 1024, b_wo[j])
            b = next_pp()
            for h in range(8):
                mm(pp[:, b, :W], ring[:, slot, h * 128:(h + 1) * 128], Abf[:, sl, h, :W], h == 0, h == 7,
                   [ring.b[slot]] + Abf.b[0:4], [pp.b[b]])
            return pp[:, b, :W], pp.b[b]
        resid_ln(W, lambda j: X0[:, sl, j, :W], lambda j: X0.b[sl * 8 + j], XC, XbfC, 0, proj_wo)

    def group(gi, tok0, W, halo, after_ffn0_hidden=None, next_wo=None):
        ffn(W, 0, XbfC, XC, XB, XbfB, 1, mid_hook=after_ffn0_hidden)
        def glu_epi(j, pb0):
            t_ = rs["t"] % 2
            rs["t"] += 1
            p.op("act", lambda e: e.activation(out=tmpa[:, t_, :W], in_=pg[:, pb0 + 1, :W], func=AF.Sigmoid,
                                               bias=vcol(VC_BPW1 + 8 + j)),
                 reads=[pg.b[pb0 + 1], vec.b[0]], writes=[tmpa.b[t_]])
            p.op("dve", lambda e: e.scalar_tensor_tensor(out=hbuf[:, j, 32:32 + W], in0=pg[:, pb0, :W], scalar=vcol(VC_BPW1 + j),
                                                         in1=tmpa[:, t_, :W], op0=ALU.add, op1=ALU.mult),
                 reads=[pg.b[pb0], tmpa.b[t_], vec.b[0]], writes=[hbuf.b[j]])

        j = 0
        while j < 8:
            nj = 2 if j == 0 else 1
            slots, pbs = [], []
            for jj in range(nj):
                slots.append(fetch(s_pw1[j + jj, :, :], 2048, b_pw1[j + jj]))
                pbs.append((rs["pg"] % 2) * 2)
                rs["pg"] += 1
            order = [(kc, jj, t) for kc in range(8) for jj in range(nj) for t in range(2)] if nj == 2 else \
                    [(kc, 0, t) for t in range(2) for kc in range(8)]
            for (kc, jj, t) in order:
                off = t * 1024 + kc * 128
                mm(pg[:, pbs[jj] + t, :W], ring[:, slots[jj], off:off + 128], XbfB[:, kc, :W], kc == 0, kc == 7,
                   [ring.b[slots[jj]], XbfB.b[kc]], [pg.b[pbs[jj] + t]])
            for jj in range(nj):
                glu_epi(j + jj, pbs[jj])
            j += nj
        if halo:
            for j in range(8):
                (lambda j: p.op("dve", lambda e: e.tensor_scalar(out=hbuf[:, j, 32:32 + W], in0=hbuf[:, j, 32:32 + W],
                                                                  scalar1=hsc[:, 0:1], scalar2=None, op0=ALU.mult),
                                reads=[hsc.b[0]], writes=[hbuf.b[j]]))(j)
        else:
            for j in range(8):
                slots = [fetch(s_dg[2 * j, :, :], 2048, b_dg[2 * j]), fetch(s_dg[2 * j + 1, :, :1920], 1920, b_dg[2 * j + 1])]
                b = next_pp()
                for k in range(CW):
                    sl_, t = slots[k // 16], k % 16
                    mm(pp[:, b, :W], ring[:, sl_, t * 128:(t + 1) * 128], hbuf[:, j, 2 + k:2 + k + W], k == 0, k == CW - 1,
                       [ring.b[sl_], hbuf.b[j], hb_tail[j]], [pp.b[b]])
                (lambda j, b: p.op("act", lambda e: e.activation(out=XA[:, j, :W], in_=pp[:, b, :W], func=AF.Identity,
                                                                 bias=vcol(VC_BDW + j)),
                                   reads=[pp.b[b], vec.b[0]], writes=[XA.b[j]]))(j, b)
        for j in range(8):
            (lambda j: p.op("pool", lambda e: e.tensor_copy(out=hbuf[:, j, 0:32], in_=hbuf[:, j, W:W + 32]),
                            reads=[hbuf.b[j]], writes=[hb_tail[j]]))(j)
        if halo:
            if next_wo is not None:
                next_wo()
            return
        pend = None
        for j in range(8):
            s = z_stats(W, XA, j)
            if pend is not None:
                stats_mm(W, *pend)
            pend = (j, s)
        stats_mm(W, *pend)

        def out_silu(j, t2, t2b):
            p.op("act", lambda e: e.activation(out=XbfA[:, j, :W], in_=t2, func=AF.Silu, scale=vcol(VC_CLNG + j), bias=vcol(VC_CLNB + j)),
                 reads=[t2b, vec.b[0]], writes=[XbfA.b[j]])
        ln_core(W, XA, None, 0, 0, out_silu)
        def proj_pw2(j):
            slot = fetch(s_pw2[j, :, :], 1024, b_pw2[j])
            b = next_pp()
            for kc in range(8):
                mm(pp[:, b, :W], ring[:, slot, kc * 128:(kc + 1) * 128], XbfA[:, kc, :W], kc == 0, kc == 7,
                   [ring.b[slot], XbfA.b[kc]], [pp.b[b]])
            return pp[:, b, :W], pp.b[b]
        resid_ln(W, lambda j: XB[:, j, :W], lambda j: XB.b[j], XA, XbfB, 2, proj_pw2, bias_col=VC_BPW2)
        o0 = tok0 - 128

        def store(j):
            p.dma("act", oT3[:, j, o0:o0 + W], XB[:, j, :W], reads=[XB.b[j]], key=("b_out", j))
        ffn(W, 1, XbfB, XA, XB, None, 3, store=store, mid_hook=next_wo)

    groups = [(96, 32, True)] + [(128 + 512 * i, 512, False) for i in range((NT - 128) // 512)]
    load_group(0, *groups[0][:2])
    stage_wo(groups[0][1])
    for gi, (tok0, W, halo) in enumerate(groups):
        hook = None
        nwo = None
        if gi + 1 < len(groups):
            hook = (lambda gi: lambda: load_group(gi + 1, *groups[gi + 1][:2]))(gi)
            nwo = (lambda gi: lambda: stage_wo(groups[gi + 1][1]))(gi)
        group(gi, tok0, W, halo, hook, nwo)


def build_prog_local(NT):
    nc = bass.Bass("TRN2", target_bir_lowering=False)
    io = declare_local_io(nc, NT)
    with ExitStack() as es:
        p = Prog(nc)
        esP = es.enter_context(ExitStack())
        scr, gen = make_prep(nc, esP, p, io)
        for _ in gen:
            pass
        p.barrier()
        esP.close()
        build_local(nc, es, p, io, NT, scr)
        p.emit(es)
    return nc


NT_LOCAL = 128 + SEQ // 2
_CACHE = {}


def _get(name, fn):
    if name not in _CACHE:
        _CACHE[name] = fn()
    return _CACHE[name]


CH_T = [(0, 2176), (2176, 2048)]
PAIRS = [[0, 1], [2, 3], [4, 5], [6, 7]]
PREP_PER_BLOCK = 4


def build_prog_fused():
    S = SEQ
    nc = bass.Bass("TRN2", target_bir_lowering=False)
    ioA = {}
    ioA["xT"] = nc.dram_tensor("xT", [D, S], F32, kind="ExternalInput").ap()
    for n in ("wq", "wk", "wv"):
        ioA[n] = nc.dram_tensor(n, [D, 512], F32, kind="ExternalInput").ap()
    ioA["rc"] = nc.dram_tensor("rc", [128, S], F32, kind="ExternalInput").ap()
    ioA["rs"] = nc.dram_tensor("rs", [128, S], F32, kind="ExternalInput").ap()
    ioA["lam"] = nc.dram_tensor("lam", [128, 256], F32, kind="ExternalInput").ap()
    ioA["subg"] = nc.dram_tensor("subg", [128, 1], F32, kind="ExternalInput").ap()
    ioB = declare_local_io(nc, NT_LOCAL, with_a=False)
    snd, rcv, sndb, rcvb = {}, {}, {}, {}
    for ps in range(2):
        for h in range(2):
            for c in range(2):
                k = (ps, h, c)
                snd[k] = nc.dram_tensor(f"snd_{ps}{h}{c}", [256, CH_T[c][1]], BF16, kind="Internal").ap()
                rcv[k] = nc.dram_tensor(f"rcv_{ps}{h}{c}", [512, CH_T[c][1]], BF16, kind="Internal").ap()
                sndb[k] = Buf(f"snd{k}")
                rcvb[k] = Buf(f"rcv{k}")
    with ExitStack() as es:
        p = Prog(nc)
        zt = Tile(p, es, "f_zero", [128, 2, 128], BF16)
        sel = Tile(p, es, "f_sel", [128, 2], F32)
        esA = es.enter_context(ExitStack())
        scr, gen = make_prep(nc, esA, p, ioB, prepq="pool")
        p.op("pool", lambda e: e.memset(zt[:], 0.0), writes=[zt.b[0]])
        for ps in range(2):
            k = (ps, 0, 0)
            p.dma("sp", snd[k].rearrange("(hh p) t -> p hh t", p=128)[:, :, 0:128], zt[:], reads=[zt.b[0]],
                  writes=[sndb[k]], key=("f_z", ps))

        def out_fn(ps, hh, g, fout, fb):
            h, kk = g // 8, g % 8
            c = 0 if kk < 4 else 1
            lt0 = 128 + 512 * kk - CH_T[c][0]
            k = (ps, h, c)
            p.dma("sp", snd[k][hh * 128:(hh + 1) * 128, lt0:lt0 + 512], fout[:, fb, :], reads=[fout.b[fb]],
                  writes=[sndb[k]], key=("a_out", fb))
            if g == 7:
                k2 = (ps, 1, 0)
                p.dma("sp", snd[k2][hh * 128:(hh + 1) * 128, 0:128], fout[:, fb, 384:512], reads=[fout.b[fb]],
                      writes=[sndb[k2]], key=("a_out2", fb))

        def block_hook(ps, g):
            for _ in range(int(math.ceil((g + 1) * 0.4))):
                next(gen, None)
            if g % 4 == 3:
                k = (ps, g // 8, (g % 8) // 4)
                p.collective((lambda k: lambda e: e.collective_compute(
                    "AllGather", ALU.bypass, replica_groups=PAIRS, ins=[snd[k][:, :]], outs=[rcv[k][:, :]]))(k),
                    reads=[sndb[k]], writes=[rcvb[k]], key=("cc",) + k)

        build_attention(nc, esA, p, S, ioA, 4, out_fn=out_fn, block_hook=block_hook)
        for _ in gen:
            pass
        p.barrier()
        esA.close()
        st = {"sel": False}

        def load_a(gi, tok0, W, Abf, hsc):
            if not st["sel"]:
                st["sel"] = True
                p.op("pool", lambda e: e.tensor_copy(out=sel[:, 1:2], in_=hsc[:, 0:1]), reads=[hsc.b[0]], writes=[sel.b[0]])
                p.op("pool", lambda e: e.tensor_scalar(out=sel[:, 0:1], in0=hsc[:, 0:1], scalar1=-1.0, scalar2=1.0,
                                                       op0=ALU.mult, op1=ALU.add), reads=[hsc.b[0]], writes=[sel.b[0]])
            c = 0 if tok0 < CH_T[1][0] else 1
            off = tok0 - CH_T[c][0]
            for h in range(2):
                for ps in range(2):
                    k = (ps, h, c)
                    v = rcv[k].rearrange("(r hh p) t -> p r hh t", r=2, hh=2, p=128)
                    for r in range(2):
                        hd = r * 4 + ps * 2
                        p.dma("sp", Abf[:, h, hd:hd + 2, :W], v[:, r, :, off:off + W], reads=[rcvb[k]],
                              writes=[Abf.b[h * 4 + ps * 2 + r]], key=("b_abf", h, ps, r))
            p.op("act", lambda e: e.activation(out=Abf[:, 0, :, :W], in_=Abf[:, 0, :, :W], func=AF.Identity, scale=sel[:, 0:1]),
                 reads=[sel.b[0]], writes=Abf.b[0:4])
            p.op("act", lambda e: e.activation(out=Abf[:, 1, :, :W], in_=Abf[:, 1, :, :W], func=AF.Identity, scale=sel[:, 1:2]),
                 reads=[sel.b[0]], writes=Abf.b[4:8])
            p.op("pool", lambda e: e.tensor_tensor(out=Abf[:, 0, :, :W], in0=Abf[:, 0, :, :W], in1=Abf[:, 1, :, :W], op=ALU.add),
                 reads=Abf.b[4:8], writes=Abf.b[0:4])

        with ExitStack() as esB:
            build_local(nc, esB, p, ioB, NT_LOCAL, scr, load_a=load_a)
            p.emit(es)
    return nc


def kernel(x, attn_w_qkv, attn_w_o, attn_lambda_q1, attn_lambda_k1, attn_lambda_q2, attn_lambda_k2, attn_subln_g,
                 conv_w_pw1, conv_b_pw1, conv_w_dw, conv_b_dw, conv_ln_g, conv_ln_b, conv_w_pw2, conv_b_pw2,
                 ffn_w_gate, ffn_w_up, ffn_w_down, ln_g, ln_b):
    f = lambda a: np.asarray(a, dtype=np.float32)
    x = f(x)
    inp = dict(ln_g=f(ln_g), ln_b=f(ln_b), conv_b_pw1=f(conv_b_pw1), conv_w_dw=f(conv_w_dw), conv_b_dw=f(conv_b_dw),
               conv_ln_g=f(conv_ln_g), conv_ln_b=f(conv_ln_b), conv_b_pw2=f(conv_b_pw2))
    B = x.shape[0]
    H = SEQ // 2
    wqkv = f(attn_w_qkv)[0]
    nc = _get("F", build_prog_fused)
    xTs = [np.ascontiguousarray(x[b].T) for b in range(B)]
    shared = {"wo": f(attn_w_o)[0], "wg0": f(ffn_w_gate)[0], "wu0": f(ffn_w_up)[0], "wd0": f(ffn_w_down)[0],
              "pw1": f(conv_w_pw1)[0], "pw2": f(conv_w_pw2)[0], "wg1": f(ffn_w_gate)[1], "wu1": f(ffn_w_up)[1],
              "wd1": f(ffn_w_down)[1], "vecs": pack_vecs(inp), "ident": np.eye(128, dtype=np.float32)}
    shared = {k: np.ascontiguousarray(v) for k, v in shared.items()}
    in_maps = []
    for c in range(8):
        b, half = c // 2, c % 2
        m = attention_inputs(x[b], wqkv, f(attn_lambda_q1)[0], f(attn_lambda_k1)[0], f(attn_lambda_q2)[0],
                             f(attn_lambda_k2)[0], f(attn_subln_g)[0], 4 * half, 4, SEQ)
        m["xT"] = xTs[b]
        m.update(shared)
        if half == 0:
            xTl = np.concatenate([np.zeros((D, 128), np.float32), xTs[b][:, :H]], axis=1)
        else:
            xTl = xTs[b][:, H - 128:]
        m["xTl"] = np.ascontiguousarray(xTl)
        m["hscale"] = np.full((128, 1), float(half), np.float32)
        in_maps.append(m)
    res = run_bass_kernel_spmd(nc, in_maps, core_ids=list(range(8)))
    out = np.empty((B, SEQ, D), np.float32)
    for c in range(8):
        b, half = c // 2, c % 2
        out[b, half * H:(half + 1) * H, :] = res.results[c]["oT"].T
    return out
```

```python
import math
import numpy as np
import ml_dtypes
from contextlib import ExitStack
import concourse.bass as bass
import concourse.mybir as mybir
from concourse.bass_utils import run_bass_kernel_spmd

F32 = mybir.dt.float32
BF16 = mybir.dt.bfloat16
AF = mybir.ActivationFunctionType
ALU = mybir.AluOpType
AX = mybir.AxisListType

D = 1024
SEQ = 8192
NH = 8
DFF = 2816
NF = DFF // 128
CW = 31
KD = 8
EPS = 1e-5
ALPHA = (2.0 * 2) ** 0.25
LAMBDA_INIT0 = 0.8 - 0.6 * math.exp(0.0)
EPOCH = 30000


class Buf:
    __slots__ = ("name", "w", "r")

    def __init__(self, name):
        self.name = name
        self.w = None
        self.r = []


class Op:
    __slots__ = ("eng", "fn", "deps", "needs_inc", "count", "is_dma", "key", "idx", "unit")

    def __init__(self, eng, fn, is_dma=False, key=None, unit=16):
        self.unit = unit
        self.eng = eng
        self.fn = fn
        self.deps = []
        self.needs_inc = False
        self.count = None
        self.is_dma = is_dma
        self.key = key


class Prog:
    ENGS = ("pe", "act", "dve", "pool", "sp")

    def __init__(self, nc):
        self.nc = nc
        self.ops = {e: [] for e in self.ENGS}
        self.dma_keys = {}
        self.dma_units = {}
        self.all_dma = []

    def _add_deps(self, op, reads, writes):
        deps = op.deps
        for b in reads:
            if b.w is not None:
                deps.append(b.w)
        for b in writes:
            if b.w is not None:
                deps.append(b.w)
            deps.extend(b.r)
        for b in reads:
            b.r.append(op)
        for b in writes:
            b.w = op
            b.r = []

    def op(self, eng, fn, reads=(), writes=()):
        o = Op(eng, fn)
        self._add_deps(o, reads, writes)
        self.ops[eng].append(o)
        return o

    def dma(self, queue, out, in_, reads=(), writes=(), key=None):
        o = Op(queue, lambda e: e.dma_start(out=out, in_=in_), is_dma=True, key=key)
        o.needs_inc = True
        self._add_deps(o, reads, writes)
        self.ops[queue].append(o)
        self.all_dma.append(o)
        return o

    def collective(self, fn, reads=(), writes=(), key=None):
        o = Op("pool", fn, is_dma=True, key=key, unit=1)
        o.needs_inc = True
        self._add_deps(o, reads, writes)
        self.ops["pool"].append(o)
        self.all_dma.append(o)
        return o

    def barrier(self):
        lasts = []
        for e in self.ENGS:
            for o in reversed(self.ops[e]):
                if not o.is_dma:
                    lasts.append(o)
                    break
        lastkey = {}
        for o in self.all_dma:
            lastkey[o.key] = o
        lasts.extend(lastkey.values())
        for e in self.ENGS:
            o = Op(e, lambda eng: eng.nop())
            o.deps = list(lasts)
            self.ops[e].append(o)

    def emit(self, es, final_wait=True):
        nc = self.nc
        for e in self.ENGS:
            for o in self.ops[e]:
                for d in o.deps:
                    d.needs_inc = True
        eng_sems = {}
        for e in self.ENGS:
            c = 0
            for o in self.ops[e]:
                if o.is_dma:
                    k = o.key
                    n = self.dma_keys.get(k, 0) + 1
                    self.dma_keys[k] = n
                    o.count = ("dma", k, n * o.unit)
                    self.dma_units[k] = o.unit
                elif o.needs_inc:
                    c += 1
                    o.count = (e, (c - 1) // EPOCH, (c - 1) % EPOCH + 1)
            nep = (c + EPOCH - 1) // EPOCH
            for i in range(max(nep, 1)):
                eng_sems[(e, i)] = es.enter_context(nc.semaphore(f"s_{e}_{i}"))
        dma_sems = {}
        for i, k in enumerate(self.dma_keys):
            dma_sems[k] = es.enter_context(nc.semaphore(f"sd_{i}"))
        self.n_sems = len(eng_sems) + len(dma_sems)

        def semof(cnt):
            if cnt[0] == "dma":
                return dma_sems[cnt[1]], cnt[2], ("dma", cnt[1])
            return eng_sems[(cnt[0], cnt[1])], cnt[2], (cnt[0], cnt[1])

        def run_engine(ename, eng):
            waited = {}
            for o in self.ops[ename]:
                need = {}
                for d in o.deps:
                    if d.eng == ename and not d.is_dma and ename == "pe":
                        continue
                    sem, val, sk = semof(d.count)
                    if waited.get(sk, 0) >= val:
                        continue
                    if need.get(sk, (None, 0))[1] < val:
                        need[sk] = (sem, val)
                items = list(need.items())
                for sk, (sem, val) in items[1:]:
                    eng.wait_ge(sem, val)
                    waited[sk] = val
                ins = o.fn(eng)
                if items:
                    sk, (sem, val) = items[0]
                    ins._wait_ge(sem, val)
                    waited[sk] = val
                if o.needs_inc:
                    sem, val, sk = semof(o.count)
                    if o.is_dma and o.unit == 1:
                        ins.then_inc(sem)
                    else:
                        ins.then_inc(sem, 16 if o.is_dma else 1)
            if ename == "sp" and final_wait:
                for k, n in self.dma_keys.items():
                    eng.wait_ge(dma_sems[k], n * self.dma_units[k])

        block = es.enter_context(nc.Block())

        @block.tensor
        def _(e):
            run_engine("pe", e)

        @block.scalar
        def _(e):
            run_engine("act", e)

        @block.vector
        def _(e):
            run_engine("dve", e)

        @block.gpsimd
        def _(e):
            run_engine("pool", e)

        @block.sync
        def _(e):
            run_engine("sp", e)


class Tile:
    def __init__(self, p, es, name, shape, dtype, space="sbuf", nbufs=1):
        nc = p.nc
        if space == "sbuf":
            self.t = es.enter_context(nc.sbuf_tensor(name, list(shape), dtype))
        else:
            self.t = es.enter_context(nc.psum_tensor(name, list(shape), dtype))
        self.b = [Buf(f"{name}.{i}") for i in range(nbufs)]
        self.shape = shape

    def __getitem__(self, idx):
        return self.t[idx]


def build_attention(nc, es, p, S, io, HPC=4, out_fn=None, block_hook=None):
    NB = S // 512
    NKT = S // 128
    PASS_H = 2
    NPASS = HPC // PASS_H
    xT = io["xT"].rearrange("(kc p) t -> p kc t", p=128)
    T = lambda name, shape, dt, space="sbuf", nbufs=1: Tile(p, es, name, shape, dt, space, nbufs)

    ident = None
    ones_bf = T("a_ones", [128, 128], BF16)
    onesm_bf = T("a_onesm", [128, 128], BF16)
    wq = T("a_wq", [128, 8, PASS_H * 128], BF16)
    wk = T("a_wk", [128, 8, PASS_H * 128], BF16)
    wv = T("a_wv", [128, 8, PASS_H * 128], BF16)
    wst = T("a_wst", [128, 8, PASS_H * 128], F32, nbufs=1)
    KT = T("a_KT", [128, PASS_H, S], BF16, nbufs=PASS_H * NB)
    V = T("a_V", [128, NKT, PASS_H * 128], BF16, nbufs=NB)
    xf = T("a_xf", [128, 2, 8, 512], F32, nbufs=2)
    xb = T("a_xb", [128, 2, 8, 512], BF16, nbufs=2)
    tabc = T("a_tabc", [128, 2, 512], F32, nbufs=2)
    tabs = T("a_tabs", [128, 2, 512], F32, nbufs=2)
    QT = T("a_QT", [128, 2, PASS_H, 512], BF16, nbufs=2 * PASS_H)
    rt1 = T("a_rt1", [128, 2, 512], F32, nbufs=2)
    rt2 = T("a_rt2", [128, 2, 512], F32, nbufs=2)
    NPT = 4
    pt = T("a_pt", [128, NPT, 2, 512], BF16, nbufs=NPT)
    NPP = 4
    ppair = T("a_ppair", [128, NPP, 2, 512], BF16, nbufs=NPP)
    lamt = T("a_lam", [128, 256], F32)
    lamp = T("a_lamp", [128, 128], F32)
    lams = T("a_lams", [128, 8], F32)
    subg = T("a_subg", [128, 2], F32)
    fr = T("a_fr", [128, 3, 512], F32, nbufs=3)
    fo = T("a_fo", [128, 2, 512], F32, nbufs=2)
    fsq = T("a_fsq", [128, 512], BF16)
    frs = T("a_frs", [128, 512], F32)
    fout = T("a_fout", [128, 2, 512], BF16, nbufs=2)
    ps_s = T("a_ps_s", [128, 2, 2, 512], F32, "psum", nbufs=2)
    ps_o = T("a_ps_o", [128, 2, 512], F32, "psum", nbufs=2)
    ps_l = T("a_ps_l", [128, 512], F32, "psum", nbufs=1)
    ps_p = T("a_ps_p", [128, 512], F32, "psum", nbufs=1)

    zmask = T("a_zmask", [128, 2, 64], BF16)
    p.op("pool", lambda e: e.memset(zmask[:], 0.0), writes=[zmask.b[0]])
    p.op("pool", lambda e: e.memset(ones_bf[:], 1.0), writes=[ones_bf.b[0]])
    p.op("pool", lambda e: e.memset(onesm_bf[:], 1.0 / 128.0), writes=[onesm_bf.b[0]])
    p.dma("sp", lamt[:], io["lam"][:, :], writes=[lamt.b[0]], key="a_misc")
    p.op("dve", lambda e: e.tensor_tensor(out=lamp[:, 0:64], in0=lamt[:, 0:64], in1=lamt[:, 64:128], op=ALU.mult),
         reads=[lamt.b[0]], writes=[lamp.b[0]])
    p.op("dve", lambda e: e.tensor_tensor(out=lamp[:, 64:128], in0=lamt[:, 128:192], in1=lamt[:, 192:256], op=ALU.mult),
         reads=[lamt.b[0]], writes=[lamp.b[0]])
    lb = Buf("lams")
    p.op("dve", lambda e: e.reduce_sum(out=lams[:, 0:1], in_=lamp[:, 0:64], axis=AX.X), reads=[lamp.b[0]], writes=[lb])
    p.op("dve", lambda e: e.reduce_sum(out=lams[:, 1:2], in_=lamp[:, 64:128], axis=AX.X), reads=[lamp.b[0]], writes=[lb])
    p.op("act", lambda e: e.activation(out=lams[:, 2:4], in_=lams[:, 0:2], func=AF.Exp), reads=[lb], writes=[lb])
    p.op("dve", lambda e: e.tensor_tensor(out=lams[:, 4:5], in0=lams[:, 3:4], in1=lams[:, 2:3], op=ALU.subtract),
         reads=[lb], writes=[lb])
    p.op("dve", lambda e: e.tensor_scalar(out=lams[:, 5:6], in0=lams[:, 4:5], scalar1=-LAMBDA_INIT0, scalar2=None,
                                          op0=ALU.add), reads=[lb], writes=[lb])
    neg_lam = lams[:, 5:6]
    p.dma("sp", subg[:, 0:1], io["subg"][:, :], writes=[subg.b[0]], key="a_misc2")
    p.op("dve", lambda e: e.tensor_scalar(out=subg[:, 1:2], in0=subg[:, 0:1], scalar1=1.0 - LAMBDA_INIT0, scalar2=None,
                                          op0=ALU.mult), reads=[subg.b[0]], writes=[subg.b[0]])
    gs = subg[:, 1:2]

    def rope_evac(src_ap, src_buf, sl, dst_ap, dst_buf, W=512):
        a1 = rt1[:, sl, :W]
        a2 = rt2[:, sl, :W]
        cc = tabc[:, sl, :W]
        p.op("dve", lambda e: e.tensor_tensor(out=a1, in0=src_ap, in1=cc, op=ALU.mult),
             reads=[src_buf, tabc.b[sl]], writes=[rt1.b[sl]])
        for (o0, i0) in ((0, 32), (32, 0), (64, 96), (96, 64)):
            p.op("dve", (lambda o0, i0: lambda e: e.tensor_tensor(
                out=rt2[o0:o0 + 32, sl, :W], in0=src_ap[i0:i0 + 32, :], in1=tabs[o0:o0 + 32, sl, :W], op=ALU.mult))(o0, i0),
                reads=[src_buf, tabs.b[sl]], writes=[rt2.b[sl]])
        p.op("dve", lambda e: e.tensor_tensor(out=dst_ap, in0=a1, in1=a2, op=ALU.add),
             reads=[rt1.b[sl], rt2.b[sl]], writes=[dst_buf])

    ao = io.get("aoT")
    for ps in range(NPASS):
        for wi, (wt, wsrc) in enumerate(((wq, io["wq"]), (wk, io["wk"]), (wv, io["wv"]))):
            src = wsrc.rearrange("(kc p) n -> p kc n", p=128)[:, :, ps * PASS_H * 128:(ps + 1) * PASS_H * 128]
            p.dma("sp", wst[:], src, writes=[wst.b[0]], key="a_wst")
            p.op("dve", (lambda wt: lambda e: e.tensor_copy(out=wt[:], in_=wst[:]))(wt), reads=[wst.b[0]], writes=[wt.b[0]])

        def load_block(g):
            sl = g % 2
            p.dma("sp", xf[:, sl], xT[:, :, g * 512:(g + 1) * 512], writes=[xf.b[sl]], key=("a_xf", sl))
            p.dma("sp", tabc[:, sl], io["rc"][:, g * 512:(g + 1) * 512], writes=[tabc.b[sl]], key=("a_tc", sl))
            p.dma("sp", tabs[:, sl], io["rs"][:, g * 512:(g + 1) * 512], writes=[tabs.b[sl]], key=("a_ts", sl))
            p.op("dve", lambda e: e.tensor_copy(out=xb[:, sl, 0:4], in_=xf[:, sl, 0:4]), reads=[xf.b[sl]], writes=[xb.b[sl]])
            p.op("dve", lambda e: e.tensor_copy(out=xb[:, sl, 4:8], in_=xf[:, sl, 4:8]), reads=[xf.b[sl]], writes=[xb.b[sl]])

        def project(g):
            sl = g % 2
            for hh in range(PASS_H):
                for kind in ("k", "q"):
                    wt = wk if kind == "k" else wq
                    for kc in range(8):
                        p.op("pe", (lambda kc, wt, hh: lambda e: e.matmul(
                            ps_p[:, :], lhsT=wt[:, kc, hh * 128:(hh + 1) * 128], rhs=xb[:, sl, kc, :],
                            start=(kc == 0), stop=(kc == 7)))(kc, wt, hh),
                            reads=[wt.b[0], xb.b[sl]], writes=[ps_p.b[0]])
                    if kind == "k":
                        rope_evac(ps_p[:, :], ps_p.b[0], sl, KT[:, hh, g * 512:(g + 1) * 512], KT.b[hh * NB + g])
                    else:
                        rope_evac(ps_p[:, :], ps_p.b[0], sl, QT[:, sl, hh, :], QT.b[sl * PASS_H + hh])
                    yield
            for half in range(2):
                for tt in range(2):
                    tok = (half * 2 + tt) * 128
                    for kc in range(8):
                        p.op("pe", (lambda kc, tt, tok: lambda e: e.matmul(
                            ps_p[:, tt * 256:(tt + 1) * 256], lhsT=xb[:, sl, kc, tok:tok + 128], rhs=wv[:, kc, :],
                            start=(kc == 0), stop=(kc == 7)))(kc, tt, tok),
                            reads=[wv.b[0], xb.b[sl]], writes=[ps_p.b[0]])
                kt0 = g * 4 + half * 2
                p.op("dve", (lambda kt0: lambda e: e.tensor_copy(
                    out=V[:, kt0:kt0 + 2, :], in_=ps_p[:, :].rearrange("p (a b) -> p a b", a=2)))(kt0),
                    reads=[ps_p.b[0]], writes=[V.b[g]])
                yield

        state = {"pt": 0, "fin": 0, "sb": 0, "pp": 0}
        LAG = 2
        SUM_LAG = 6

        def run_fin2():
            f2 = state.get("fin2")
            if f2 is not None:
                state["fin2"] = None
                f2()
            if state.get("post") is not None:
                pg = state["post"]
                state["post"] = None
                if block_hook is not None:
                    block_hook(*pg)

        def attend(g, hh, bg, bg_every):
            nkt = 4 * g + 4
            q_ap = QT[:, g % 2, hh]
            q_buf = QT.b[(g % 2) * PASS_H + hh]
            pend = []
            spend = []

            def pv(kt, slot, c0):
                last = (kt == nkt - 1)
                for s in range(2):
                    p.op("pe", (lambda s: lambda e: e.matmul(
                        ps_o[:, s, c0:512], lhsT=V[:, kt, hh * 128:(hh + 1) * 128], rhs=pt[:, slot, s, c0:512],
                        start=(kt == 0), stop=last, skip_group_check=True))(s),
                        reads=[V.b[kt // 4], pt.b[slot]], writes=[ps_o.b[s]])

            def sums(kt, slot, c0, sum_src):
                last = (kt == nkt - 1)
                if sum_src[0] == "pt":
                    src, sbuf_, first = pt[:, slot], pt.b[slot], (kt == 0)
                else:
                    src, sbuf_, first = ppair[:, sum_src[1]], ppair.b[sum_src[1]], (sum_src[2] == 0)
                for s in range(2):
                    p.op("pe", (lambda s: lambda e: e.matmul(
                        ps_l[64 * s:64 * s + 64, c0:512], lhsT=ones_bf[:, 0:64], rhs=src[:, s, c0:512],
                        start=first, stop=last, skip_group_check=True, tile_position=(0, 64 * s)))(s),
                        reads=[ones_bf.b[0], sbuf_], writes=[ps_l.b[0]])

            def score(kt, c0, slot, diag, sb):
                kb = KT.b[hh * NB + kt // 4]
                for s in range(2):
                    p.op("pe", (lambda s: lambda e: e.matmul(
                        ps_s[:, sb, s, c0:512], lhsT=KT[64 * s:64 * s + 64, hh, kt * 128:(kt + 1) * 128],
                        rhs=q_ap[64 * s:64 * s + 64, c0:512], start=True, stop=True))(s),
                        reads=[kb, q_buf], writes=[ps_s.b[sb]])
                p.op("act", lambda e: e.activation(
                    out=pt[:, slot, :, c0:512], in_=ps_s[:, sb, :, c0:512], func=AF.Exp, scale=0.125),
                    reads=[ps_s.b[sb]], writes=[pt.b[slot]])
                if diag:
                    p.op("act", lambda e: e.activation(out=pt[64:128, slot, :, c0:c0 + 64], in_=zmask[64:128, :, :], func=AF.Identity),
                         reads=[zmask.b[0]], writes=[pt.b[slot]])

            for kt in range(nkt):
                j = kt - 4 * g
                c0 = 128 * j if j > 0 else 0
                slot = state["pt"] % NPT
                state["pt"] += 1
                sb = state["sb"] % 2
                state["sb"] += 1
                score(kt, c0, slot, j >= 0, sb)
                if j < 0 and kt % 2 == 0:
                    sum_src = None
                    prev_slot = slot
                elif j < 0:
                    ps_ = state["pp"] % NPP
                    state["pp"] += 1
                    (lambda a, b_, ps_: p.op("dve", lambda e: e.tensor_tensor(out=ppair[:, ps_], in0=pt[:, a], in1=pt[:, b_], op=ALU.add),
                                             reads=[pt.b[a], pt.b[b_]], writes=[ppair.b[ps_]]))(prev_slot, slot, ps_)
                    sum_src = ("pair", ps_, kt - 1)
                else:
                    sum_src = ("pt", slot)
                if len(pend) >= LAG:
                    pv(*pend.pop(0))
                pend.append((kt, slot, c0))
                while spend and (kt - spend[0][0] >= (SUM_LAG if spend[0][3][0] == "pair" else LAG)):
                    sums(*spend.pop(0))
                if sum_src is not None:
                    spend.append((kt, slot, c0, sum_src))
                if bg is not None and bg_every and (kt % bg_every) == bg_every - 1:
                    next(bg, None)
                if kt == 2:
                    run_fin2()
            while pend:
                pv(*pend.pop(0))
            while spend:
                sums(*spend.pop(0))
            fb = state["fin"] % 2
            state["fin"] += 1
            fb2 = 1 - fb
            p.op("act", lambda e: e.activation(out=fr[:, 0, :], in_=ps_l[:, :], func=AF.Ln), reads=[ps_l.b[0]], writes=[fr.b[0]])
            p.op("dve", lambda e: e.tensor_copy(out=fo[:, fb, :], in_=ps_o[:, 0, :]), reads=[ps_o.b[0]], writes=[fo.b[fb]])
            p.op("dve", lambda e: e.tensor_copy(out=fo[:, fb2, :], in_=ps_o[:, 1, :]), reads=[ps_o.b[1]], writes=[fo.b[fb2]])
            for (o0, i0, dst) in ((0, 0, 1), (64, 0, 1), (0, 64, 2), (64, 64, 2)):
                p.op("act", (lambda o0, i0, dst: lambda e: e.activation(out=fr[o0:o0 + 64, dst, :], in_=fr[i0:i0 + 64, 0, :],
                                                                        func=AF.Exp, scale=-1.0))(o0, i0, dst),
                     reads=[fr.b[0]], writes=[fr.b[dst]])
            p.op("dve", lambda e: e.tensor_tensor(out=fo[:, fb, :], in0=fo[:, fb, :], in1=fr[:, 1, :], op=ALU.mult),
                 reads=[fr.b[1]], writes=[fo.b[fb]])
            p.op("dve", lambda e: e.scalar_tensor_tensor(out=fo[:, fb2, :], in0=fo[:, fb2, :], scalar=neg_lam, in1=fr[:, 2, :],
                                                         op0=ALU.mult, op1=ALU.mult),
                 reads=[fr.b[2], lb], writes=[fo.b[fb2]])
            p.op("dve", lambda e: e.tensor_tensor(out=fo[:, fb, :], in0=fo[:, fb, :], in1=fo[:, fb2, :], op=ALU.add),
                 reads=[fo.b[fb2]], writes=[fo.b[fb]])
            p.op("dve", lambda e: e.tensor_tensor(out=fsq[:], in0=fo[:, fb, :], in1=fo[:, fb, :], op=ALU.mult),
                 reads=[fo.b[fb]], writes=[fsq.b[0]])
            def fin2():
                p.op("pe", lambda e: e.matmul(ps_p[:, :], lhsT=onesm_bf[:], rhs=fsq[:], start=True, stop=True),
                     reads=[onesm_bf.b[0], fsq.b[0]], writes=[ps_p.b[0]])
                p.op("act", lambda e: e.activation(out=frs[:], in_=ps_p[:, :], func=AF.Ln, bias=EPS),
                     reads=[ps_p.b[0]], writes=[frs.b[0]])
                p.op("act", lambda e: e.activation(out=frs[:], in_=frs[:], func=AF.Exp, scale=-0.5),
                     reads=[frs.b[0]], writes=[frs.b[0]])
                p.op("dve", lambda e: e.scalar_tensor_tensor(out=fout[:, fb, :], in0=fo[:, fb, :], scalar=gs, in1=frs[:],
                                                             op0=ALU.mult, op1=ALU.mult),
                     reads=[fo.b[fb], frs.b[0], subg.b[0]], writes=[fout.b[fb]])
                if out_fn is not None:
                    out_fn(ps, hh, g, fout, fb)
                else:
                    hrow = (ps * PASS_H + hh) * 128
                    p.dma("sp", ao[hrow:hrow + 128, g * 512:(g + 1) * 512], fout[:, fb, :], reads=[fout.b[fb]],
                          key=("a_out", fb))
            state["fin2"] = fin2

        load_block(0)
        for _ in project(0):
            pass
        for g in range(NB):
            bg = None
            if g + 1 < NB:
                load_block(g + 1)
                bg = project(g + 1)
            nsteps = (4 * g + 4) * PASS_H
            bg_every = max(1, nsteps // 8)
            for hh in range(PASS_H):
                attend(g, hh, bg, bg_every)
            if bg is not None:
                for _ in bg:
                    pass
            state["post"] = (ps, g)
            if g == NB - 1:
                run_fin2()


def rope_tables_np(S):
    pos = np.arange(S, dtype=np.float32)
    inv_freq = (10000.0 ** (-np.arange(0, 64, 2, dtype=np.float32) / 64)).astype(np.float32)
    ang = pos[None, :] * inv_freq[:, None]
    c = np.cos(ang).astype(np.float32)
    s = np.sin(ang).astype(np.float32)
    rc = np.concatenate([c, c, c, c], axis=0)
    rs = np.concatenate([-s, s, -s, s], axis=0)
    return np.ascontiguousarray(rc), np.ascontiguousarray(rs)


def build_prog_attention(S=SEQ, HPC=4):
    nc = bass.Bass("TRN2", target_bir_lowering=False)
    io = {}
    io["xT"] = nc.dram_tensor("xT", [D, S], F32, kind="ExternalInput").ap()
    for n in ("wq", "wk", "wv"):
        io[n] = nc.dram_tensor(n, [D, HPC * 128], F32, kind="ExternalInput").ap()
    io["rc"] = nc.dram_tensor("rc", [128, S], F32, kind="ExternalInput").ap()
    io["rs"] = nc.dram_tensor("rs", [128, S], F32, kind="ExternalInput").ap()
    io["lam"] = nc.dram_tensor("lam", [128, 256], F32, kind="ExternalInput").ap()
    io["subg"] = nc.dram_tensor("subg", [128, 1], F32, kind="ExternalInput").ap()
    io["aoT"] = nc.dram_tensor("aoT", [HPC * 128, S], BF16, kind="ExternalOutput").ap()
    with ExitStack() as es:
        p = Prog(nc)
        build_attention(nc, es, p, S, io, HPC)
        p.emit(es)
    return nc


def attention_inputs(x_b, w_qkv, lq1, lk1, lq2, lk2, subg, h0, HPC, S):
    rc, rs = rope_tables_np(S)
    lam = np.concatenate([lq1, lk1, lq2, lk2]).astype(np.float32)[None, :].repeat(128, axis=0)
    return {
        "xT": np.ascontiguousarray(x_b[:S].T),
        "wq": np.ascontiguousarray(w_qkv[:, h0 * 128:(h0 + HPC) * 128]),
        "wk": np.ascontiguousarray(w_qkv[:, D + h0 * 128:D + (h0 + HPC) * 128]),
        "wv": np.ascontiguousarray(w_qkv[:, 2 * D + h0 * 128:2 * D + (h0 + HPC) * 128]),
        "rc": rc, "rs": rs, "lam": np.ascontiguousarray(lam),
        "subg": np.ascontiguousarray(subg.reshape(128, 1).astype(np.float32)),
    }


VC_LN = 0
VC_BPW1 = 64
VC_WDW = 80
VC_BDW = 328
VC_CLNG = 336
VC_CLNB = 344
VC_BPW2 = 352
NVEC = 360
RING_ELEMS = 2944
NSLOT = 6


def pack_vecs(inp):
    cols = lambda v: np.asarray(v, np.float32).reshape(-1, 128).T
    out = np.zeros((128, NVEC), np.float32)
    for i in range(2):
        for which in range(2):
            base = VC_LN + ((i * 2 + which) * 2) * 8
            out[:, base:base + 8] = cols(inp["ln_g"][i, which])
            out[:, base + 8:base + 16] = cols(inp["ln_b"][i, which])
    out[:, VC_BPW1:VC_BPW1 + 16] = cols(inp["conv_b_pw1"][0])
    wdw = np.asarray(inp["conv_w_dw"][0], np.float32).reshape(CW, 8, 128).transpose(2, 0, 1).reshape(128, CW * 8)
    out[:, VC_WDW:VC_WDW + CW * 8] = wdw
    out[:, VC_BDW:VC_BDW + 8] = cols(inp["conv_b_dw"][0])
    out[:, VC_CLNG:VC_CLNG + 8] = cols(inp["conv_ln_g"][0])
    out[:, VC_CLNB:VC_CLNB + 8] = cols(inp["conv_ln_b"][0])
    out[:, VC_BPW2:VC_BPW2 + 8] = cols(inp["conv_b_pw2"][0])
    return out


def declare_local_io(nc, NT, with_a=True):
    io = {}
    if with_a:
        io["aT"] = nc.dram_tensor("aT", [D, NT], BF16, kind="ExternalInput").ap()
    io["xTl"] = nc.dram_tensor("xTl", [D, NT], F32, kind="ExternalInput").ap()
    for n, shp in (("wo", [D, D]), ("wg0", [D, DFF]), ("wu0", [D, DFF]), ("wd0", [DFF, D]), ("pw1", [D, 2 * D]),
                   ("pw2", [D, D]), ("wg1", [D, DFF]), ("wu1", [D, DFF]), ("wd1", [DFF, D])):
        io[n] = nc.dram_tensor(n, shp, F32, kind="ExternalInput").ap()
    io["vecs"] = nc.dram_tensor("vecs", [128, NVEC], F32, kind="ExternalInput").ap()
    io["ident"] = nc.dram_tensor("ident", [128, 128], F32, kind="ExternalInput").ap()
    io["hscale"] = nc.dram_tensor("hscale", [128, 1], F32, kind="ExternalInput").ap()
    io["oT"] = nc.dram_tensor("oT", [D, NT - 128], F32, kind="ExternalOutput").ap()
    return io


def make_prep(nc, es, p, io, prepq="act"):
    T = lambda name, shape, dt, space="sbuf", nbufs=1: Tile(p, es, name, shape, dt, space, nbufs)
    def scratch(name, nj, per):
        t = nc.dram_tensor(name, [nj, 128, per], BF16, kind="Internal").ap()
        return t, [Buf(f"{name}.{j}") for j in range(nj)]
    s_wo, b_wo = scratch("s_wo", 8, 1024)
    s_gu = [scratch("s_gu0", NF, 2048), scratch("s_gu1", NF, 2048)]
    s_wd = [scratch("s_wd0", 8, NF * 128), scratch("s_wd1", 8, NF * 128)]
    s_pw1, b_pw1 = scratch("s_pw1", 8, 2048)
    s_pw2, b_pw2 = scratch("s_pw2", 8, 1024)

    s_dg, b_dg = scratch("s_dg", 16, 2048)
    stg = T("b_stg", [128, 2, 1408], F32, nbufs=2)
    stb = T("b_stb", [128, 2, 1408], BF16, nbufs=2)
    ident = T("b_ident", [128, 128], F32)
    pvec = T("b_pvec", [128, CW * 8], F32)
    dgs = T("b_dgs", [128, 1, 2048], BF16, nbufs=1)
    p.dma("sp", ident[:], io["ident"][:, :], writes=[ident.b[0]], key="b_ident")
    p.dma("sp", pvec[:], io["vecs"][:, VC_WDW:VC_WDW + CW * 8], writes=[pvec.b[0]], key="b_pvec")
    dst_ = {"i": 0}

    def prep_diag(j, half):
        sl = 0
        k0 = KD + 16 * half
        nk = 16 if half == 0 else CW - KD - 16
        for t in range(nk):
            k = k0 + t
            (lambda t, k: p.op("dve", lambda e: e.tensor_scalar(out=dgs[:, sl, t * 128:(t + 1) * 128], in0=ident[:],
                                                                scalar1=pvec[:, k * 8 + j:k * 8 + j + 1], scalar2=None, op0=ALU.mult),
                               reads=[ident.b[0], pvec.b[0]], writes=[dgs.b[sl]]))(t, k)
        p.dma(prepq, s_dg[2 * j + half, :, :nk * 128], dgs[:, sl, :nk * 128], reads=[dgs.b[sl]], writes=[b_dg[2 * j + half]],
              key=("b_dgs", sl))
    pst = {"i": 0}

    pend = []

    def flush():
        for sl, (src_ap, nel, dst_ap, dst_buf) in enumerate(pend):
            kc = nel // 128
            p.dma(prepq, stg[:, sl, :nel].rearrange("p (k c) -> p k c", k=kc), src_ap, writes=[stg.b[sl]], key=("b_stg", sl))
        for sl, (src_ap, nel, dst_ap, dst_buf) in enumerate(pend):
            (lambda sl, nel: p.op("pool", lambda e: e.tensor_copy(out=stb[:, sl, :nel], in_=stg[:, sl, :nel]),
                                  reads=[stg.b[sl]], writes=[stb.b[sl]]))(sl, nel)
            p.dma(prepq, dst_ap, stb[:, sl, :nel], reads=[stb.b[sl]], writes=[dst_buf], key=("b_stb", sl))
        pend.clear()

    def prep_unit(src_ap, nel, dst_ap, dst_buf):
        pend.append((src_ap, nel, dst_ap, dst_buf))
        if len(pend) == 2:
            flush()

    def prep_k1024(w_ap, col0, dst_scr, j, off, dst_buf):
        src = w_ap.rearrange("(kc p) n -> p kc n", p=128)[:, :, col0:col0 + 128]
        prep_unit(src, 1024, dst_scr[j, :, off:off + 1024], dst_buf)

    def prep_wd(w_ap, dst_scr, j, dst_buf):
        v = w_ap.rearrange("(kc p) n -> p kc n", p=128)
        for half in range(2):
            prep_unit(v[:, half * 11:(half + 1) * 11, j * 128:(j + 1) * 128], 1408,
                      dst_scr[j, :, half * 1408:(half + 1) * 1408], dst_buf)

    def prep_gen():
        for j in range(8):
            prep_k1024(io["wo"], j * 128, s_wo, j, 0, b_wo[j])
            yield
        for j in range(8):
            for half in range(2):
                prep_diag(j, half)
                yield
        for i in range(2):
            if i == 1:
                for j in range(8):
                    prep_k1024(io["pw1"], j * 128, s_pw1, j, 0, b_pw1[j])
                    prep_k1024(io["pw1"], D + j * 128, s_pw1, j, 1024, b_pw1[j])
                    yield
                for j in range(8):
                    prep_k1024(io["pw2"], j * 128, s_pw2, j, 0, b_pw2[j])
                    yield
            for j in range(NF):
                prep_k1024(io[f"wg{i}"], j * 128, s_gu[i][0], j, 0, s_gu[i][1][j])
                prep_k1024(io[f"wu{i}"], j * 128, s_gu[i][0], j, 1024, s_gu[i][1][j])
                yield
            for j in range(8):
                prep_wd(io[f"wd{i}"], s_wd[i][0], j, s_wd[i][1][j])
                yield
        flush()

    scr = dict(s_wo=s_wo, b_wo=b_wo, s_gu=s_gu, s_wd=s_wd, s_pw1=s_pw1, b_pw1=b_pw1, s_pw2=s_pw2, b_pw2=b_pw2,
               s_dg=s_dg, b_dg=b_dg)
    return scr, prep_gen()


def build_local(nc, es, p, io, NT, scr, load_a=None):
    T = lambda name, shape, dt, space="sbuf", nbufs=1: Tile(p, es, name, shape, dt, space, nbufs)
    s_wo, b_wo, s_gu, s_wd = scr["s_wo"], scr["b_wo"], scr["s_gu"], scr["s_wd"]
    s_pw1, b_pw1, s_pw2, b_pw2 = scr["s_pw1"], scr["b_pw1"], scr["s_pw2"], scr["b_pw2"]
    s_dg, b_dg = scr["s_dg"], scr["b_dg"]
    vec = T("b_vec", [128, NVEC], F32)
    hsc = T("b_hsc", [128, 1], F32)
    onesm = T("b_onesm", [128, 128], BF16)
    ring = T("b_ring", [128, NSLOT, RING_ELEMS], BF16, nbufs=NSLOT)
    X0 = T("b_x0", [128, 1, 8, 512], F32, nbufs=8)
    Abf = T("b_abf", [128, 2, 8, 512], BF16, nbufs=8)
    XA = T("b_xa", [128, 8, 512], F32, nbufs=8)
    XB = T("b_xb", [128, 8, 512], F32, nbufs=8)
    XbfA = T("b_xbfa", [128, 8, 512], BF16, nbufs=8)
    XbfB = T("b_xbfb", [128, 8, 512], BF16, nbufs=8)
    XC = T("b_xc", [128, 8, 512], F32, nbufs=8)
    XbfC = T("b_xbfc", [128, 8, 512], BF16, nbufs=8)
    zb = T("b_zb", [128, 3, 512], BF16, nbufs=3)
    zsq = T("b_zsq", [128, 3, 512], BF16, nbufs=3)
    hT = T("b_hT", [128, NF, 512], BF16, nbufs=NF)
    hbuf = T("b_hbuf", [128, 8, 544], BF16, nbufs=8)
    hb_tail = [Buf(f"hb_tail{j}") for j in range(8)]
    tmpa = T("b_tmpa", [128, 2, 512], F32, nbufs=2)
    tn1 = T("b_tn1", [128, 2, 512], F32, nbufs=2)
    tn2 = T("b_tn2", [128, 2, 512], F32, nbufs=2)
    st_msq = T("b_msq", [128, 512], F32)
    st_mean = T("b_mean", [128, 512], F32)
    st_var = T("b_var", [128, 512], F32)
    st_rstd = T("b_rstd", [128, 512], F32)
    pg = T("b_pg", [128, 4, 512], F32, "psum", nbufs=4)
    pp = T("b_pp", [128, 2, 512], F32, "psum", nbufs=2)
    pst_ = T("b_pst", [128, 2, 512], F32, "psum", nbufs=2)

    p.op("pool", lambda e: e.memset(onesm[:], 1.0 / 1024.0), writes=[onesm.b[0]])
    p.dma("sp", vec[:], io["vecs"][:, :], writes=[vec.b[0]], key="b_vec")
    p.dma("sp", hsc[:], io["hscale"][:, :], writes=[hsc.b[0]], key="b_hsc")
    vcol = lambda c: vec[:, c:c + 1]

    rs = {"i": 0, "z": 0, "t": 0, "n": 0, "pp": 0, "pg": 0}

    def fetch(src_ap, nel, src_buf):
        slot = rs["i"] % NSLOT
        rs["i"] += 1
        p.dma("sp", ring[:, slot, :nel], src_ap, reads=[src_buf], writes=[ring.b[slot]], key=("b_ring", slot))
        return slot

    def mm(out, lhsT, rhs, start, stop, reads, writes):
        p.op("pe", lambda e: e.matmul(out, lhsT=lhsT, rhs=rhs, start=start, stop=stop), reads=reads, writes=writes)

    xT3 = io["xTl"].rearrange("(kc p) t -> p kc t", p=128)
    aT3 = io["aT"].rearrange("(h p) t -> p h t", p=128) if "aT" in io else None
    oT3 = io["oT"].rearrange("(kc p) t -> p kc t", p=128)

    def load_group(gi, tok0, W):
        sl = 0
        p.dma("sp", X0[:, sl, :, :W], xT3[:, :, tok0:tok0 + W], writes=[X0.b[sl * 8 + j] for j in range(8)], key=("b_x0", sl))
        if load_a is not None:
            load_a(gi, tok0, W, Abf, hsc)
        else:
            p.dma("sp", Abf[:, sl, :, :W], aT3[:, :, tok0:tok0 + W], writes=Abf.b[0:4], key=("b_abf", sl))

    def ln_core(W, dstx, src_is_dst_bufs, gcol, bcol, out_fn):
        p.op("act", lambda e: e.activation(out=st_msq[:, :W], in_=pst_[:, 0, :W], func=AF.Square),
             reads=[pst_.b[0]], writes=[st_msq.b[0]])
        p.op("act", lambda e: e.activation(out=st_mean[:, :W], in_=pst_[:, 0, :W], func=AF.Identity),
             reads=[pst_.b[0]], writes=[st_mean.b[0]])
        p.op("dve", lambda e: e.tensor_tensor(out=st_var[:, :W], in0=pst_[:, 1, :W], in1=st_msq[:, :W], op=ALU.subtract),
             reads=[pst_.b[1], st_msq.b[0]], writes=[st_var.b[0]])
        p.op("act", lambda e: e.activation(out=st_var[:, :W], in_=st_var[:, :W], func=AF.Ln, bias=EPS),
             reads=[st_var.b[0]], writes=[st_var.b[0]])
        p.op("act", lambda e: e.activation(out=st_rstd[:, :W], in_=st_var[:, :W], func=AF.Exp, scale=-0.5),
             reads=[st_var.b[0]], writes=[st_rstd.b[0]])
        for j in range(8):
            s1 = rs["n"] % 2
            rs["n"] += 1
            (lambda j, s1: (
                p.op("pool" if j % 2 == 0 else "dve",
                     lambda e: e.tensor_tensor(out=tn1[:, s1, :W], in0=dstx[:, j, :W], in1=st_mean[:, :W], op=ALU.subtract),
                     reads=[dstx.b[j], st_mean.b[0]], writes=[tn1.b[s1]]),
                p.op("dve", lambda e: e.tensor_tensor(out=tn2[:, s1, :W], in0=tn1[:, s1, :W], in1=st_rstd[:, :W], op=ALU.mult),
                     reads=[tn1.b[s1], st_rstd.b[0]], writes=[tn2.b[s1]]),
                out_fn(j, tn2[:, s1, :W], tn2.b[s1])))(j, s1)

    def stats_mm(W, j, s):
        mm(pst_[:, 0, :W], onesm[:], zb[:, s, :W], j == 0, j == 7, [onesm.b[0], zb.b[s]], [pst_.b[0]])
        mm(pst_[:, 1, :W], onesm[:], zsq[:, s, :W], j == 0, j == 7, [onesm.b[0], zsq.b[s]], [pst_.b[1]])

    def z_stats(W, dstx, j):
        s = rs["z"] % 3
        rs["z"] += 1
        p.op("dve", lambda e: e.tensor_copy(out=zb[:, s, :W], in_=dstx[:, j, :W]), reads=[dstx.b[j]], writes=[zb.b[s]])
        p.op("act", lambda e: e.activation(out=zsq[:, s, :W], in_=dstx[:, j, :W], func=AF.Square), reads=[dstx.b[j]], writes=[zsq.b[s]])
        return s

    def resid_ln(W, srcx_ap_fn, srcx_buf_fn, dstx, dstbf, lnidx, proj_chunk, bias_col=None, store=None):
        gcol = VC_LN + lnidx * 16
        bcol = gcol + 8
        pend = None
        for j in range(8):
            pa, pb = proj_chunk(j)
            if bias_col is not None:
                t = rs["t"] % 2
                rs["t"] += 1
                (lambda j, t, pa, pb: p.op("act", lambda e: e.activation(out=tmpa[:, t, :W], in_=pa, func=AF.Identity,
                                                                         bias=vcol(bias_col + j)),
                                           reads=[pb, vec.b[0]], writes=[tmpa.b[t]]))(j, t, pa, pb)
                pa, pb = tmpa[:, t, :W], tmpa.b[t]
            (lambda j, pa, pb: p.op("dve", lambda e: e.scalar_tensor_tensor(
                out=dstx[:, j, :W], in0=srcx_ap_fn(j), scalar=ALPHA, in1=pa, op0=ALU.mult, op1=ALU.add),
                reads=[srcx_buf_fn(j), pb], writes=[dstx.b[j]]))(j, pa, pb)
            s = z_stats(W, dstx, j)
            if pend is not None:
                stats_mm(W, *pend)
            pend = (j, s)
        stats_mm(W, *pend)

        def out_fn(j, t2, t2b):
            if dstbf is not None:
                p.op("act", lambda e: e.activation(out=dstbf[:, j, :W], in_=t2, func=AF.Identity, scale=vcol(gcol + j), bias=vcol(bcol + j)),
                     reads=[t2b, vec.b[0]], writes=[dstbf.b[j]])
            p.op("act", lambda e: e.activation(out=dstx[:, j, :W], in_=t2, func=AF.Identity, scale=vcol(gcol + j), bias=vcol(bcol + j)),
                 reads=[t2b, vec.b[0]], writes=[dstx.b[j]])
            if store is not None:
                store(j)
        ln_core(W, dstx, None, gcol, bcol, out_fn)

    def next_pp():
        b = rs["pp"] % 2
        rs["pp"] += 1
        return b

    def ffn(W, i, xbf, srcx, dstx, dstbf, lnidx, store=None, mid_hook=None):
        s_g, b_g = s_gu[i]
        s_d, b_d = s_wd[i]

        def ffn_epi(j, pb0):
            t_ = rs["t"] % 2
            rs["t"] += 1
            p.op("act", lambda e: e.activation(out=tmpa[:, t_, :W], in_=pg[:, pb0, :W], func=AF.Silu),
                 reads=[pg.b[pb0]], writes=[tmpa.b[t_]])
            p.op("dve", lambda e: e.tensor_tensor(out=hT[:, j, :W], in0=tmpa[:, t_, :W], in1=pg[:, pb0 + 1, :W], op=ALU.mult),
                 reads=[tmpa.b[t_], pg.b[pb0 + 1]], writes=[hT.b[j]])

        j = 0
        while j < NF:
            nj = 2 if j == 0 else 1
            slots, pbs = [], []
            for jj in range(nj):
                slots.append(fetch(s_g[j + jj, :, :], 2048, b_g[j + jj]))
                pbs.append((rs["pg"] % 2) * 2)
                rs["pg"] += 1
            if nj == 2:
                for kc in range(8):
                    for jj in range(2):
                        for t in range(2):
                            off = t * 1024 + kc * 128
                            mm(pg[:, pbs[jj] + t, :W], ring[:, slots[jj], off:off + 128], xbf[:, kc, :W], kc == 0, kc == 7,
                               [ring.b[slots[jj]], xbf.b[kc]], [pg.b[pbs[jj] + t]])
            else:
                for t in range(2):
                    for kc in range(8):
                        off = t * 1024 + kc * 128
                        mm(pg[:, pbs[0] + t, :W], ring[:, slots[0], off:off + 128], xbf[:, kc, :W], kc == 0, kc == 7,
                           [ring.b[slots[0]], xbf.b[kc]], [pg.b[pbs[0] + t]])
            for jj in range(nj):
                ffn_epi(j + jj, pbs[jj])
            j += nj

        if mid_hook is not None:
            mid_hook()

        def proj_chunk(j):
            slot = fetch(s_d[j, :, :], NF * 128, b_d[j])
            b = next_pp()
            for kc in range(NF):
                mm(pp[:, b, :W], ring[:, slot, kc * 128:(kc + 1) * 128], hT[:, kc, :W], kc == 0, kc == NF - 1,
                   [ring.b[slot], hT.b[kc]], [pp.b[b]])
            return pp[:, b, :W], pp.b[b]
        resid_ln(W, lambda j: srcx[:, j, :W], lambda j: srcx.b[j], dstx, dstbf, lnidx, proj_chunk, store=store)

    def stage_wo(W):
        sl = 0

        def proj_wo(j):
            slot = fetch(s_wo[j, :, :], 1024, b_wo[j])
            b = next_pp()
            for h in range(8):
                mm(pp[:, b, :W], ring[:, slot, h * 128:(h + 1) * 128], Abf[:, sl, h, :W], h == 0, h == 7,
                   [ring.b[slot]] + Abf.b[0:4], [pp.b[b]])
            return pp[:, b, :W], pp.b[b]
        resid_ln(W, lambda j: X0[:, sl, j, :W], lambda j: X0.b[sl * 8 + j], XC, XbfC, 0, proj_wo)

    def group(gi, tok0, W, halo, after_ffn0_hidden=None, next_wo=None):
        ffn(W, 0, XbfC, XC, XB, XbfB, 1, mid_hook=after_ffn0_hidden)
        def glu_epi(j, pb0):
            t_ = rs["t"] % 2
            rs["t"] += 1
            p.op("act", lambda e: e.activation(out=tmpa[:, t_, :W], in_=pg[:, pb0 + 1, :W], func=AF.Sigmoid,
                                               bias=vcol(VC_BPW1 + 8 + j)),
                 reads=[pg.b[pb0 + 1], vec.b[0]], writes=[tmpa.b[t_]])
            p.op("dve", lambda e: e.scalar_tensor_tensor(out=hbuf[:, j, 32:32 + W], in0=pg[:, pb0, :W], scalar=vcol(VC_BPW1 + j),
                                                         in1=tmpa[:, t_, :W], op0=ALU.add, op1=ALU.mult),
                 reads=[pg.b[pb0], tmpa.b[t_], vec.b[0]], writes=[hbuf.b[j]])

        j = 0
        while j < 8:
            nj = 2 if j == 0 else 1
            slots, pbs = [], []
            for jj in range(nj):
                slots.append(fetch(s_pw1[j + jj, :, :], 2048, b_pw1[j + jj]))
                pbs.append((rs["pg"] % 2) * 2)
                rs["pg"] += 1
            order = [(kc, jj, t) for kc in range(8) for jj in range(nj) for t in range(2)] if nj == 2 else \
                    [(kc, 0, t) for t in range(2) for kc in range(8)]
            for (kc, jj, t) in order:
                off = t * 1024 + kc * 128
                mm(pg[:, pbs[jj] + t, :W], ring[:, slots[jj], off:off + 128], XbfB[:, kc, :W], kc == 0, kc == 7,
                   [ring.b[slots[jj]], XbfB.b[kc]], [pg.b[pbs[jj] + t]])
            for jj in range(nj):
                glu_epi(j + jj, pbs[jj])
            j += nj
        if halo:
            for j in range(8):
                (lambda j: p.op("dve", lambda e: e.tensor_scalar(out=hbuf[:, j, 32:32 + W], in0=hbuf[:, j, 32:32 + W],
                                                                  scalar1=hsc[:, 0:1], scalar2=None, op0=ALU.mult),
                                reads=[hsc.b[0]], writes=[hbuf.b[j]]))(j)
        else:
            n1 = CW - KD - 16
            for jp in range(0, 8, 2):
                for k in range(KD):
                    for j in (jp, jp + 1):
                        wc = vcol(VC_WDW + k * 8 + j)
                        src = hbuf[:, j, 2 + k:2 + k + W]
                        if k == 0:
                            (lambda j, wc, src: p.op("dve", lambda e: e.tensor_scalar(
                                out=XA[:, j, :W], in0=src, scalar1=wc, scalar2=None, op0=ALU.mult),
                                reads=[hbuf.b[j], hb_tail[j], vec.b[0]], writes=[XA.b[j]]))(j, wc, src)
                        else:
                            (lambda j, wc, src: p.op("dve", lambda e: e.scalar_tensor_tensor(
                                out=XA[:, j, :W], in0=src, scalar=wc, in1=XA[:, j, :W], op0=ALU.mult, op1=ALU.add),
                                reads=[hbuf.b[j], hb_tail[j], vec.b[0]], writes=[XA.b[j]]))(j, wc, src)
                for j in (jp, jp + 1):
                    slots = [fetch(s_dg[2 * j, :, :], 2048, b_dg[2 * j]),
                             fetch(s_dg[2 * j + 1, :, :n1 * 128], n1 * 128, b_dg[2 * j + 1])]
                    b = next_pp()
                    for k in range(KD, CW):
                        sl_, t = slots[(k - KD) // 16], (k - KD) % 16
                        mm(pp[:, b, :W], ring[:, sl_, t * 128:(t + 1) * 128], hbuf[:, j, 2 + k:2 + k + W], k == KD, k == CW - 1,
                           [ring.b[sl_], hbuf.b[j], hb_tail[j]], [pp.b[b]])
                    (lambda j, b: p.op("dve", lambda e: e.scalar_tensor_tensor(
                        out=XA[:, j, :W], in0=pp[:, b, :W], scalar=vcol(VC_BDW + j), in1=XA[:, j, :W], op0=ALU.add, op1=ALU.add),
                        reads=[pp.b[b], vec.b[0]], writes=[XA.b[j]]))(j, b)
        for j in range(8):
            (lambda j: p.op("pool", lambda e: e.tensor_copy(out=hbuf[:, j, 0:32], in_=hbuf[:, j, W:W + 32]),
                            reads=[hbuf.b[j]], writes=[hb_tail[j]]))(j)
        if halo:
            if next_wo is not None:
                next_wo()
            return
        pend = None
        for j in range(8):
            s = z_stats(W, XA, j)
            if pend is not None:
                stats_mm(W, *pend)
            pend = (j, s)
        stats_mm(W, *pend)

        def out_silu(j, t2, t2b):
            p.op("act", lambda e: e.activation(out=XbfA[:, j, :W], in_=t2, func=AF.Silu, scale=vcol(VC_CLNG + j), bias=vcol(VC_CLNB + j)),
                 reads=[t2b, vec.b[0]], writes=[XbfA.b[j]])
        ln_core(W, XA, None, 0, 0, out_silu)
        def proj_pw2(j):
            slot = fetch(s_pw2[j, :, :], 1024, b_pw2[j])
            b = next_pp()
            for kc in range(8):
                mm(pp[:, b, :W], ring[:, slot, kc * 128:(kc + 1) * 128], XbfA[:, kc, :W], kc == 0, kc == 7,
                   [ring.b[slot], XbfA.b[kc]], [pp.b[b]])
            return pp[:, b, :W], pp.b[b]
        resid_ln(W, lambda j: XB[:, j, :W], lambda j: XB.b[j], XA, XbfB, 2, proj_pw2, bias_col=VC_BPW2)
        o0 = tok0 - 128

        def store(j):
            p.dma("act", oT3[:, j, o0:o0 + W], XB[:, j, :W], reads=[XB.b[j]], key=("b_out", j))
        ffn(W, 1, XbfB, XA, XB, None, 3, store=store, mid_hook=next_wo)

    groups = [(96, 32, True)] + [(128 + 512 * i, 512, False) for i in range((NT - 128) // 512)]
    load_group(0, *groups[0][:2])
    stage_wo(groups[0][1])
    for gi, (tok0, W, halo) in enumerate(groups):
        hook = None
        nwo = None
        if gi + 1 < len(groups):
            hook = (lambda gi: lambda: load_group(gi + 1, *groups[gi + 1][:2]))(gi)
            nwo = (lambda gi: lambda: stage_wo(groups[gi + 1][1]))(gi)
        group(gi, tok0, W, halo, hook, nwo)


def build_prog_local(NT):
    nc = bass.Bass("TRN2", target_bir_lowering=False)
    io = declare_local_io(nc, NT)
    with ExitStack() as es:
        p = Prog(nc)
        esP = es.enter_context(ExitStack())
        scr, gen = make_prep(nc, esP, p, io)
        for _ in gen:
            pass
        p.barrier()
        esP.close()
        build_local(nc, es, p, io, NT, scr)
        p.emit(es)
    return nc


NT_LOCAL = 128 + SEQ // 2
_CACHE = {}


def _get(name, fn):
    if name not in _CACHE:
        _CACHE[name] = fn()
    return _CACHE[name]


CH_T = [(0, 2176), (2176, 2048)]
PAIRS = [[0, 1], [2, 3], [4, 5], [6, 7]]
PREP_PER_BLOCK = 4


def build_prog_fused():
    S = SEQ
    nc = bass.Bass("TRN2", target_bir_lowering=False)
    ioA = {}
    ioA["xT"] = nc.dram_tensor("xT", [D, S], F32, kind="ExternalInput").ap()
    for n in ("wq", "wk", "wv"):
        ioA[n] = nc.dram_tensor(n, [D, 512], F32, kind="ExternalInput").ap()
    ioA["rc"] = nc.dram_tensor("rc", [128, S], F32, kind="ExternalInput").ap()
    ioA["rs"] = nc.dram_tensor("rs", [128, S], F32, kind="ExternalInput").ap()
    ioA["lam"] = nc.dram_tensor("lam", [128, 256], F32, kind="ExternalInput").ap()
    ioA["subg"] = nc.dram_tensor("subg", [128, 1], F32, kind="ExternalInput").ap()
    ioB = declare_local_io(nc, NT_LOCAL, with_a=False)
    snd, rcv, sndb, rcvb = {}, {}, {}, {}
    for ps in range(2):
        for h in range(2):
            for c in range(2):
                k = (ps, h, c)
                snd[k] = nc.dram_tensor(f"snd_{ps}{h}{c}", [256, CH_T[c][1]], BF16, kind="Internal").ap()
                rcv[k] = nc.dram_tensor(f"rcv_{ps}{h}{c}", [512, CH_T[c][1]], BF16, kind="Internal").ap()
                sndb[k] = Buf(f"snd{k}")
                rcvb[k] = Buf(f"rcv{k}")
    with ExitStack() as es:
        p = Prog(nc)
        zt = Tile(p, es, "f_zero", [128, 2, 128], BF16)
        sel = Tile(p, es, "f_sel", [128, 2], F32)
        esA = es.enter_context(ExitStack())
        scr, gen = make_prep(nc, esA, p, ioB, prepq="pool")
        p.op("pool", lambda e: e.memset(zt[:], 0.0), writes=[zt.b[0]])
        for ps in range(2):
            k = (ps, 0, 0)
            p.dma("sp", snd[k].rearrange("(hh p) t -> p hh t", p=128)[:, :, 0:128], zt[:], reads=[zt.b[0]],
                  writes=[sndb[k]], key=("f_z", ps))

        def out_fn(ps, hh, g, fout, fb):
            h, kk = g // 8, g % 8
            c = 0 if kk < 4 else 1
            lt0 = 128 + 512 * kk - CH_T[c][0]
            k = (ps, h, c)
            p.dma("sp", snd[k][hh * 128:(hh + 1) * 128, lt0:lt0 + 512], fout[:, fb, :], reads=[fout.b[fb]],
                  writes=[sndb[k]], key=("a_out", fb))
            if g == 7:
                k2 = (ps, 1, 0)
                p.dma("sp", snd[k2][hh * 128:(hh + 1) * 128, 0:128], fout[:, fb, 384:512], reads=[fout.b[fb]],
                      writes=[sndb[k2]], key=("a_out2", fb))

        def block_hook(ps, g):
            for _ in range(int(math.ceil((g + 1) * 0.4))):
                next(gen, None)
            if g % 4 == 3:
                k = (ps, g // 8, (g % 8) // 4)
                p.collective((lambda k: lambda e: e.collective_compute(
                    "AllGather", ALU.bypass, replica_groups=PAIRS, ins=[snd[k][:, :]], outs=[rcv[k][:, :]]))(k),
                    reads=[sndb[k]], writes=[rcvb[k]], key=("cc",) + k)

        build_attention(nc, esA, p, S, ioA, 4, out_fn=out_fn, block_hook=block_hook)
        for _ in gen:
            pass
        p.barrier()
        esA.close()
        st = {"sel": False}

        def load_a(gi, tok0, W, Abf, hsc):
            if not st["sel"]:
                st["sel"] = True
                p.op("pool", lambda e: e.tensor_copy(out=sel[:, 1:2], in_=hsc[:, 0:1]), reads=[hsc.b[0]], writes=[sel.b[0]])
                p.op("pool", lambda e: e.tensor_scalar(out=sel[:, 0:1], in0=hsc[:, 0:1], scalar1=-1.0, scalar2=1.0,
                                                       op0=ALU.mult, op1=ALU.add), reads=[hsc.b[0]], writes=[sel.b[0]])
            c = 0 if tok0 < CH_T[1][0] else 1
            off = tok0 - CH_T[c][0]
            for h in range(2):
                for ps in range(2):
                    k = (ps, h, c)
                    v = rcv[k].rearrange("(r hh p) t -> p r hh t", r=2, hh=2, p=128)
                    for r in range(2):
                        hd = r * 4 + ps * 2
                        p.dma("sp", Abf[:, h, hd:hd + 2, :W], v[:, r, :, off:off + W], reads=[rcvb[k]],
                              writes=[Abf.b[h * 4 + ps * 2 + r]], key=("b_abf", h, ps, r))
            p.op("act", lambda e: e.activation(out=Abf[:, 0, :, :W], in_=Abf[:, 0, :, :W], func=AF.Identity, scale=sel[:, 0:1]),
                 reads=[sel.b[0]], writes=Abf.b[0:4])
            p.op("act", lambda e: e.activation(out=Abf[:, 1, :, :W], in_=Abf[:, 1, :, :W], func=AF.Identity, scale=sel[:, 1:2]),
                 reads=[sel.b[0]], writes=Abf.b[4:8])
            p.op("pool", lambda e: e.tensor_tensor(out=Abf[:, 0, :, :W], in0=Abf[:, 0, :, :W], in1=Abf[:, 1, :, :W], op=ALU.add),
                 reads=Abf.b[4:8], writes=Abf.b[0:4])

        with ExitStack() as esB:
            build_local(nc, esB, p, ioB, NT_LOCAL, scr, load_a=load_a)
            p.emit(es)
    return nc


def kernel(x, attn_w_qkv, attn_w_o, attn_lambda_q1, attn_lambda_k1, attn_lambda_q2, attn_lambda_k2, attn_subln_g,
                 conv_w_pw1, conv_b_pw1, conv_w_dw, conv_b_dw, conv_ln_g, conv_ln_b, conv_w_pw2, conv_b_pw2,
                 ffn_w_gate, ffn_w_up, ffn_w_down, ln_g, ln_b):
    f = lambda a: np.asarray(a, dtype=np.float32)
    x = f(x)
    inp = dict(ln_g=f(ln_g), ln_b=f(ln_b), conv_b_pw1=f(conv_b_pw1), conv_w_dw=f(conv_w_dw), conv_b_dw=f(conv_b_dw),
               conv_ln_g=f(conv_ln_g), conv_ln_b=f(conv_ln_b), conv_b_pw2=f(conv_b_pw2))
    B = x.shape[0]
    H = SEQ // 2
    wqkv = f(attn_w_qkv)[0]
    nc = _get("F", build_prog_fused)
    xTs = [np.ascontiguousarray(x[b].T) for b in range(B)]
    shared = {"wo": f(attn_w_o)[0], "wg0": f(ffn_w_gate)[0], "wu0": f(ffn_w_up)[0], "wd0": f(ffn_w_down)[0],
              "pw1": f(conv_w_pw1)[0], "pw2": f(conv_w_pw2)[0], "wg1": f(ffn_w_gate)[1], "wu1": f(ffn_w_up)[1],
              "wd1": f(ffn_w_down)[1], "vecs": pack_vecs(inp), "ident": np.eye(128, dtype=np.float32)}
    shared = {k: np.ascontiguousarray(v) for k, v in shared.items()}
    in_maps = []
    for c in range(8):
        b, half = c // 2, c % 2
        m = attention_inputs(x[b], wqkv, f(attn_lambda_q1)[0], f(attn_lambda_k1)[0], f(attn_lambda_q2)[0],
                             f(attn_lambda_k2)[0], f(attn_subln_g)[0], 4 * half, 4, SEQ)
        m["xT"] = xTs[b]
        m.update(shared)
        if half == 0:
            xTl = np.concatenate([np.zeros((D, 128), np.float32), xTs[b][:, :H]], axis=1)
        else:
            xTl = xTs[b][:, H - 128:]
        m["xTl"] = np.ascontiguousarray(xTl)
        m["hscale"] = np.full((128, 1), float(half), np.float32)
        in_maps.append(m)
    res = run_bass_kernel_spmd(nc, in_maps, core_ids=list(range(8)))
    out = np.empty((B, SEQ, D), np.float32)
    for c in range(8):
        b, half = c // 2, c % 2
        out[b, half * H:(half + 1) * H, :] = res.results[c]["oT"].T
    return out
```

```python
import math
import numpy as np
import ml_dtypes
from contextlib import ExitStack
import concourse.bass as bass
import concourse.mybir as mybir
from concourse.bass_utils import run_bass_kernel_spmd

F32 = mybir.dt.float32
BF16 = mybir.dt.bfloat16
AF = mybir.ActivationFunctionType
ALU = mybir.AluOpType
AX = mybir.AxisListType

D = 1024
SEQ = 8192
NH = 8
DFF = 2816
NF = DFF // 128
CW = 31
KD = 8
EPS = 1e-5
ALPHA = (2.0 * 2) ** 0.25
LAMBDA_INIT0 = 0.8 - 0.6 * math.exp(0.0)
EPOCH = 30000


class Buf:
    __slots__ = ("name", "w", "r")

    def __init__(self, name):
        self.name = name
        self.w = None
        self.r = []


class Op:
    __slots__ = ("eng", "fn", "deps", "needs_inc", "count", "is_dma", "key", "idx", "unit")

    def __init__(self, eng, fn, is_dma=False, key=None, unit=16):
        self.unit = unit
        self.eng = eng
        self.fn = fn
        self.deps = []
        self.needs_inc = False
        self.count = None
        self.is_dma = is_dma
        self.key = key


class Prog:
    ENGS = ("pe", "act", "dve", "pool", "sp")

    def __init__(self, nc):
        self.nc = nc
        self.ops = {e: [] for e in self.ENGS}
        self.dma_keys = {}
        self.dma_units = {}
        self.all_dma = []

    def _add_deps(self, op, reads, writes):
        deps = op.deps
        for b in reads:
            if b.w is not None:
                deps.append(b.w)
        for b in writes:
            if b.w is not None:
                deps.append(b.w)
            deps.extend(b.r)
        for b in reads:
            b.r.append(op)
        for b in writes:
            b.w = op
            b.r = []

    def op(self, eng, fn, reads=(), writes=()):
        o = Op(eng, fn)
        self._add_deps(o, reads, writes)
        self.ops[eng].append(o)
        return o

    def dma(self, queue, out, in_, reads=(), writes=(), key=None):
        o = Op(queue, lambda e: e.dma_start(out=out, in_=in_), is_dma=True, key=key)
        o.needs_inc = True
        self._add_deps(o, reads, writes)
        self.ops[queue].append(o)
        self.all_dma.append(o)
        return o

    def collective(self, fn, reads=(), writes=(), key=None):
        o = Op("pool", fn, is_dma=True, key=key, unit=1)
        o.needs_inc = True
        self._add_deps(o, reads, writes)
        self.ops["pool"].append(o)
        self.all_dma.append(o)
        return o

    def barrier(self):
        lasts = []
        for e in self.ENGS:
            for o in reversed(self.ops[e]):
                if not o.is_dma:
                    lasts.append(o)
                    break
        lastkey = {}
        for o in self.all_dma:
            lastkey[o.key] = o
        lasts.extend(lastkey.values())
        for e in self.ENGS:
            o = Op(e, lambda eng: eng.nop())
            o.deps = list(lasts)
            self.ops[e].append(o)

    def emit(self, es, final_wait=True):
        nc = self.nc
        for e in self.ENGS:
            for o in self.ops[e]:
                for d in o.deps:
                    d.needs_inc = True
        eng_sems = {}
        for e in self.ENGS:
            c = 0
            for o in self.ops[e]:
                if o.is_dma:
                    k = o.key
                    n = self.dma_keys.get(k, 0) + 1
                    self.dma_keys[k] = n
                    o.count = ("dma", k, n * o.unit)
                    self.dma_units[k] = o.unit
                elif o.needs_inc:
                    c += 1
                    o.count = (e, (c - 1) // EPOCH, (c - 1) % EPOCH + 1)
            nep = (c + EPOCH - 1) // EPOCH
            for i in range(max(nep, 1)):
                eng_sems[(e, i)] = es.enter_context(nc.semaphore(f"s_{e}_{i}"))
        dma_sems = {}
        for i, k in enumerate(self.dma_keys):
            dma_sems[k] = es.enter_context(nc.semaphore(f"sd_{i}"))
        self.n_sems = len(eng_sems) + len(dma_sems)

        def semof(cnt):
            if cnt[0] == "dma":
                return dma_sems[cnt[1]], cnt[2], ("dma", cnt[1])
            return eng_sems[(cnt[0], cnt[1])], cnt[2], (cnt[0], cnt[1])

        def run_engine(ename, eng):
            waited = {}
            for o in self.ops[ename]:
                need = {}
                for d in o.deps:
                    if d.eng == ename and not d.is_dma and ename == "pe":
                        continue
                    sem, val, sk = semof(d.count)
                    if waited.get(sk, 0) >= val:
                        continue
                    if need.get(sk, (None, 0))[1] < val:
                        need[sk] = (sem, val)
                items = list(need.items())
                for sk, (sem, val) in items[1:]:
                    eng.wait_ge(sem, val)
                    waited[sk] = val
                ins = o.fn(eng)
                if items:
                    sk, (sem, val) = items[0]
                    ins._wait_ge(sem, val)
                    waited[sk] = val
                if o.needs_inc:
                    sem, val, sk = semof(o.count)
                    if o.is_dma and o.unit == 1:
                        ins.then_inc(sem)
                    else:
                        ins.then_inc(sem, 16 if o.is_dma else 1)
            if ename == "sp" and final_wait:
                for k, n in self.dma_keys.items():
                    eng.wait_ge(dma_sems[k], n * self.dma_units[k])

        block = es.enter_context(nc.Block())

        @block.tensor
        def _(e):
            run_engine("pe", e)

        @block.scalar
        def _(e):
            run_engine("act", e)

        @block.vector
        def _(e):
            run_engine("dve", e)

        @block.gpsimd
        def _(e):
            run_engine("pool", e)

        @block.sync
        def _(e):
            run_engine("sp", e)


class Tile:
    def __init__(self, p, es, name, shape, dtype, space="sbuf", nbufs=1):
        nc = p.nc
        if space == "sbuf":
            self.t = es.enter_context(nc.sbuf_tensor(name, list(shape), dtype))
        else:
            self.t = es.enter_context(nc.psum_tensor(name, list(shape), dtype))
        self.b = [Buf(f"{name}.{i}") for i in range(nbufs)]
        self.shape = shape

    def __getitem__(self, idx):
        return self.t[idx]


def build_attention(nc, es, p, S, io, HPC=4, out_fn=None, block_hook=None):
    NB = S // 512
    NKT = S // 128
    PASS_H = 2
    NPASS = HPC // PASS_H
    xT = io["xT"].rearrange("(kc p) t -> p kc t", p=128)
    T = lambda name, shape, dt, space="sbuf", nbufs=1: Tile(p, es, name, shape, dt, space, nbufs)

    ident = None
    ones_bf = T("a_ones", [128, 128], BF16)
    onesm_bf = T("a_onesm", [128, 128], BF16)
    wq = T("a_wq", [128, 8, PASS_H * 128], BF16)
    wk = T("a_wk", [128, 8, PASS_H * 128], BF16)
    wv = T("a_wv", [128, 8, PASS_H * 128], BF16)
    wst = T("a_wst", [128, 8, PASS_H * 128], F32, nbufs=1)
    KT = T("a_KT", [128, PASS_H, S], BF16, nbufs=PASS_H * NB)
    V = T("a_V", [128, NKT, PASS_H * 128], BF16, nbufs=NB)
    xf = T("a_xf", [128, 2, 8, 512], F32, nbufs=2)
    xb = T("a_xb", [128, 2, 8, 512], BF16, nbufs=2)
    tabc = T("a_tabc", [128, 2, 512], F32, nbufs=2)
    tabs = T("a_tabs", [128, 2, 512], F32, nbufs=2)
    QT = T("a_QT", [128, 2, PASS_H, 512], BF16, nbufs=2 * PASS_H)
    rt1 = T("a_rt1", [128, 2, 512], F32, nbufs=2)
    rt2 = T("a_rt2", [128, 2, 512], F32, nbufs=2)
    NPT = 4
    pt = T("a_pt", [128, NPT, 2, 512], BF16, nbufs=NPT)
    NPP = 4
    ppair = T("a_ppair", [128, NPP, 2, 512], BF16, nbufs=NPP)
    lamt = T("a_lam", [128, 256], F32)
    lamp = T("a_lamp", [128, 128], F32)
    lams = T("a_lams", [128, 8], F32)
    subg = T("a_subg", [128, 2], F32)
    fr = T("a_fr", [128, 3, 512], F32, nbufs=3)
    fo = T("a_fo", [128, 2, 512], F32, nbufs=2)
    fsq = T("a_fsq", [128, 512], BF16)
    frs = T("a_frs", [128, 512], F32)
    fout = T("a_fout", [128, 2, 512], BF16, nbufs=2)
    ps_s = T("a_ps_s", [128, 2, 2, 512], F32, "psum", nbufs=2)
    ps_o = T("a_ps_o", [128, 2, 512], F32, "psum", nbufs=2)
    ps_l = T("a_ps_l", [128, 512], F32, "psum", nbufs=1)
    ps_p = T("a_ps_p", [128, 512], F32, "psum", nbufs=1)

    zmask = T("a_zmask", [128, 2, 64], BF16)
    p.op("pool", lambda e: e.memset(zmask[:], 0.0), writes=[zmask.b[0]])
    p.op("pool", lambda e: e.memset(ones_bf[:], 1.0), writes=[ones_bf.b[0]])
    p.op("pool", lambda e: e.memset(onesm_bf[:], 1.0 / 128.0), writes=[onesm_bf.b[0]])
    p.dma("sp", lamt[:], io["lam"][:, :], writes=[lamt.b[0]], key="a_misc")
    p.op("dve", lambda e: e.tensor_tensor(out=lamp[:, 0:64], in0=lamt[:, 0:64], in1=lamt[:, 64:128], op=ALU.mult),
         reads=[lamt.b[0]], writes=[lamp.b[0]])
    p.op("dve", lambda e: e.tensor_tensor(out=lamp[:, 64:128], in0=lamt[:, 128:192], in1=lamt[:, 192:256], op=ALU.mult),
         reads=[lamt.b[0]], writes=[lamp.b[0]])
    lb = Buf("lams")
    p.op("dve", lambda e: e.reduce_sum(out=lams[:, 0:1], in_=lamp[:, 0:64], axis=AX.X), reads=[lamp.b[0]], writes=[lb])
    p.op("dve", lambda e: e.reduce_sum(out=lams[:, 1:2], in_=lamp[:, 64:128], axis=AX.X), reads=[lamp.b[0]], writes=[lb])
    p.op("act", lambda e: e.activation(out=lams[:, 2:4], in_=lams[:, 0:2], func=AF.Exp), reads=[lb], writes=[lb])
    p.op("dve", lambda e: e.tensor_tensor(out=lams[:, 4:5], in0=lams[:, 3:4], in1=lams[:, 2:3], op=ALU.subtract),
         reads=[lb], writes=[lb])
    p.op("dve", lambda e: e.tensor_scalar(out=lams[:, 5:6], in0=lams[:, 4:5], scalar1=-LAMBDA_INIT0, scalar2=None,
                                          op0=ALU.add), reads=[lb], writes=[lb])
    neg_lam = lams[:, 5:6]
    p.dma("sp", subg[:, 0:1], io["subg"][:, :], writes=[subg.b[0]], key="a_misc2")
    p.op("dve", lambda e: e.tensor_scalar(out=subg[:, 1:2], in0=subg[:, 0:1], scalar1=1.0 - LAMBDA_INIT0, scalar2=None,
                                          op0=ALU.mult), reads=[subg.b[0]], writes=[subg.b[0]])
    gs = subg[:, 1:2]

    def rope_evac(src_ap, src_buf, sl, dst_ap, dst_buf, W=512):
        a1 = rt1[:, sl, :W]
        a2 = rt2[:, sl, :W]
        cc = tabc[:, sl, :W]
        p.op("dve", lambda e: e.tensor_tensor(out=a1, in0=src_ap, in1=cc, op=ALU.mult),
             reads=[src_buf, tabc.b[sl]], writes=[rt1.b[sl]])
        for (o0, i0) in ((0, 32), (32, 0), (64, 96), (96, 64)):
            p.op("dve", (lambda o0, i0: lambda e: e.tensor_tensor(
                out=rt2[o0:o0 + 32, sl, :W], in0=src_ap[i0:i0 + 32, :], in1=tabs[o0:o0 + 32, sl, :W], op=ALU.mult))(o0, i0),
                reads=[src_buf, tabs.b[sl]], writes=[rt2.b[sl]])
        p.op("dve", lambda e: e.tensor_tensor(out=dst_ap, in0=a1, in1=a2, op=ALU.add),
             reads=[rt1.b[sl], rt2.b[sl]], writes=[dst_buf])

    ao = io.get("aoT")
    for ps in range(NPASS):
        for wi, (wt, wsrc) in enumerate(((wq, io["wq"]), (wk, io["wk"]), (wv, io["wv"]))):
            src = wsrc.rearrange("(kc p) n -> p kc n", p=128)[:, :, ps * PASS_H * 128:(ps + 1) * PASS_H * 128]
            p.dma("sp", wst[:], src, writes=[wst.b[0]], key="a_wst")
            p.op("dve", (lambda wt: lambda e: e.tensor_copy(out=wt[:], in_=wst[:]))(wt), reads=[wst.b[0]], writes=[wt.b[0]])

        def load_block(g):
            sl = g % 2
            p.dma("sp", xf[:, sl], xT[:, :, g * 512:(g + 1) * 512], writes=[xf.b[sl]], key=("a_xf", sl))
            p.dma("sp", tabc[:, sl], io["rc"][:, g * 512:(g + 1) * 512], writes=[tabc.b[sl]], key=("a_tc", sl))
            p.dma("sp", tabs[:, sl], io["rs"][:, g * 512:(g + 1) * 512], writes=[tabs.b[sl]], key=("a_ts", sl))
            p.op("dve", lambda e: e.tensor_copy(out=xb[:, sl, 0:4], in_=xf[:, sl, 0:4]), reads=[xf.b[sl]], writes=[xb.b[sl]])
            p.op("dve", lambda e: e.tensor_copy(out=xb[:, sl, 4:8], in_=xf[:, sl, 4:8]), reads=[xf.b[sl]], writes=[xb.b[sl]])

        def project(g):
            sl = g % 2
            for hh in range(PASS_H):
                for kind in ("k", "q"):
                    wt = wk if kind == "k" else wq
                    for kc in range(8):
                        p.op("pe", (lambda kc, wt, hh: lambda e: e.matmul(
                            ps_p[:, :], lhsT=wt[:, kc, hh * 128:(hh + 1) * 128], rhs=xb[:, sl, kc, :],
                            start=(kc == 0), stop=(kc == 7)))(kc, wt, hh),
                            reads=[wt.b[0], xb.b[sl]], writes=[ps_p.b[0]])
                    if kind == "k":
                        rope_evac(ps_p[:, :], ps_p.b[0], sl, KT[:, hh, g * 512:(g + 1) * 512], KT.b[hh * NB + g])
                    else:
                        rope_evac(ps_p[:, :], ps_p.b[0], sl, QT[:, sl, hh, :], QT.b[sl * PASS_H + hh])
                    yield
            for half in range(2):
                for tt in range(2):
                    tok = (half * 2 + tt) * 128
                    for kc in range(8):
                        p.op("pe", (lambda kc, tt, tok: lambda e: e.matmul(
                            ps_p[:, tt * 256:(tt + 1) * 256], lhsT=xb[:, sl, kc, tok:tok + 128], rhs=wv[:, kc, :],
                            start=(kc == 0), stop=(kc == 7)))(kc, tt, tok),
                            reads=[wv.b[0], xb.b[sl]], writes=[ps_p.b[0]])
                kt0 = g * 4 + half * 2
                p.op("dve", (lambda kt0: lambda e: e.tensor_copy(
                    out=V[:, kt0:kt0 + 2, :], in_=ps_p[:, :].rearrange("p (a b) -> p a b", a=2)))(kt0),
                    reads=[ps_p.b[0]], writes=[V.b[g]])
                yield

        state = {"pt": 0, "fin": 0, "sb": 0, "pp": 0}
        LAG = 2
        SUM_LAG = 6

        def run_fin2():
            f2 = state.get("fin2")
            if f2 is not None:
                state["fin2"] = None
                f2()
            if state.get("post") is not None:
                pg = state["post"]
                state["post"] = None
                if block_hook is not None:
                    block_hook(*pg)

        def attend(g, hh, bg, bg_every):
            nkt = 4 * g + 4
            q_ap = QT[:, g % 2, hh]
            q_buf = QT.b[(g % 2) * PASS_H + hh]
            pend = []
            spend = []

            def pv(kt, slot, c0):
                last = (kt == nkt - 1)
                for s in range(2):
                    p.op("pe", (lambda s: lambda e: e.matmul(
                        ps_o[:, s, c0:512], lhsT=V[:, kt, hh * 128:(hh + 1) * 128], rhs=pt[:, slot, s, c0:512],
                        start=(kt == 0), stop=last, skip_group_check=True))(s),
                        reads=[V.b[kt // 4], pt.b[slot]], writes=[ps_o.b[s]])

            def sums(kt, slot, c0, sum_src):
                last = (kt == nkt - 1)
                if sum_src[0] == "pt":
                    src, sbuf_, first = pt[:, slot], pt.b[slot], (kt == 0)
                else:
                    src, sbuf_, first = ppair[:, sum_src[1]], ppair.b[sum_src[1]], (sum_src[2] == 0)
                for s in range(2):
                    p.op("pe", (lambda s: lambda e: e.matmul(
                        ps_l[64 * s:64 * s + 64, c0:512], lhsT=ones_bf[:, 0:64], rhs=src[:, s, c0:512],
                        start=first, stop=last, skip_group_check=True, tile_position=(0, 64 * s)))(s),
                        reads=[ones_bf.b[0], sbuf_], writes=[ps_l.b[0]])

            def score(kt, c0, slot, diag, sb):
                kb = KT.b[hh * NB + kt // 4]
                for s in range(2):
                    p.op("pe", (lambda s: lambda e: e.matmul(
                        ps_s[:, sb, s, c0:512], lhsT=KT[64 * s:64 * s + 64, hh, kt * 128:(kt + 1) * 128],
                        rhs=q_ap[64 * s:64 * s + 64, c0:512], start=True, stop=True))(s),
                        reads=[kb, q_buf], writes=[ps_s.b[sb]])
                p.op("act", lambda e: e.activation(
                    out=pt[:, slot, :, c0:512], in_=ps_s[:, sb, :, c0:512], func=AF.Exp, scale=0.125),
                    reads=[ps_s.b[sb]], writes=[pt.b[slot]])
                if diag:
                    p.op("act", lambda e: e.activation(out=pt[64:128, slot, :, c0:c0 + 64], in_=zmask[64:128, :, :], func=AF.Identity),
                         reads=[zmask.b[0]], writes=[pt.b[slot]])

            for kt in range(nkt):
                j = kt - 4 * g
                c0 = 128 * j if j > 0 else 0
                slot = state["pt"] % NPT
                state["pt"] += 1
                sb = state["sb"] % 2
                state["sb"] += 1
                score(kt, c0, slot, j >= 0, sb)
                if j < 0 and kt % 2 == 0:
                    sum_src = None
                    prev_slot = slot
                elif j < 0:
                    ps_ = state["pp"] % NPP
                    state["pp"] += 1
                    (lambda a, b_, ps_: p.op("dve", lambda e: e.tensor_tensor(out=ppair[:, ps_], in0=pt[:, a], in1=pt[:, b_], op=ALU.add),
                                             reads=[pt.b[a], pt.b[b_]], writes=[ppair.b[ps_]]))(prev_slot, slot, ps_)
                    sum_src = ("pair", ps_, kt - 1)
                else:
                    sum_src = ("pt", slot)
                if len(pend) >= LAG:
                    pv(*pend.pop(0))
                pend.append((kt, slot, c0))
                while spend and (kt - spend[0][0] >= (SUM_LAG if spend[0][3][0] == "pair" else LAG)):
                    sums(*spend.pop(0))
                if sum_src is not None:
                    spend.append((kt, slot, c0, sum_src))
                if bg is not None and bg_every and (kt % bg_every) == bg_every - 1:
                    next(bg, None)
                if kt == min(6, nkt - 2):
                    run_fin2()
            while pend:
                pv(*pend.pop(0))
            while spend:
                sums(*spend.pop(0))
            fb = state["fin"] % 2
            state["fin"] += 1
            fb2 = 1 - fb
            p.op("act", lambda e: e.activation(out=fr[:, 0, :], in_=ps_l[:, :], func=AF.Ln), reads=[ps_l.b[0]], writes=[fr.b[0]])
            p.op("dve", lambda e: e.tensor_copy(out=fo[:, fb, :], in_=ps_o[:, 0, :]), reads=[ps_o.b[0]], writes=[fo.b[fb]])
            p.op("dve", lambda e: e.tensor_copy(out=fo[:, fb2, :], in_=ps_o[:, 1, :]), reads=[ps_o.b[1]], writes=[fo.b[fb2]])
            for (o0, i0, dst) in ((0, 0, 1), (64, 0, 1), (0, 64, 2), (64, 64, 2)):
                p.op("act", (lambda o0, i0, dst: lambda e: e.activation(out=fr[o0:o0 + 64, dst, :], in_=fr[i0:i0 + 64, 0, :],
                                                                        func=AF.Exp, scale=-1.0))(o0, i0, dst),
                     reads=[fr.b[0]], writes=[fr.b[dst]])
            p.op("dve", lambda e: e.tensor_tensor(out=fo[:, fb, :], in0=fo[:, fb, :], in1=fr[:, 1, :], op=ALU.mult),
                 reads=[fr.b[1]], writes=[fo.b[fb]])
            p.op("dve", lambda e: e.scalar_tensor_tensor(out=fo[:, fb2, :], in0=fo[:, fb2, :], scalar=neg_lam, in1=fr[:, 2, :],
                                                         op0=ALU.mult, op1=ALU.mult),
                 reads=[fr.b[2], lb], writes=[fo.b[fb2]])
            p.op("dve", lambda e: e.tensor_tensor(out=fo[:, fb, :], in0=fo[:, fb, :], in1=fo[:, fb2, :], op=ALU.add),
                 reads=[fo.b[fb2]], writes=[fo.b[fb]])
            p.op("dve", lambda e: e.tensor_tensor(out=fsq[:], in0=fo[:, fb, :], in1=fo[:, fb, :], op=ALU.mult),
                 reads=[fo.b[fb]], writes=[fsq.b[0]])
            def fin2():
                p.op("pe", lambda e: e.matmul(ps_p[:, :], lhsT=onesm_bf[:], rhs=fsq[:], start=True, stop=True),
                     reads=[onesm_bf.b[0], fsq.b[0]], writes=[ps_p.b[0]])
                p.op("act", lambda e: e.activation(out=frs[:], in_=ps_p[:, :], func=AF.Ln, bias=EPS),
                     reads=[ps_p.b[0]], writes=[frs.b[0]])
                p.op("act", lambda e: e.activation(out=frs[:], in_=frs[:], func=AF.Exp, scale=-0.5),
                     reads=[frs.b[0]], writes=[frs.b[0]])
                p.op("dve", lambda e: e.scalar_tensor_tensor(out=fout[:, fb, :], in0=fo[:, fb, :], scalar=gs, in1=frs[:],
                                                             op0=ALU.mult, op1=ALU.mult),
                     reads=[fo.b[fb], frs.b[0], subg.b[0]], writes=[fout.b[fb]])
                if out_fn is not None:
                    out_fn(ps, hh, g, fout, fb)
                else:
                    hrow = (ps * PASS_H + hh) * 128
                    p.dma("sp", ao[hrow:hrow + 128, g * 512:(g + 1) * 512], fout[:, fb, :], reads=[fout.b[fb]],
                          key=("a_out", fb))
            state["fin2"] = fin2

        load_block(0)
        for _ in project(0):
            pass
        for g in range(NB):
            bg = None
            if g + 1 < NB:
                load_block(g + 1)
                bg = project(g + 1)
            nsteps = (4 * g + 4) * PASS_H
            bg_every = max(1, nsteps // 8)
            for hh in range(PASS_H):
                attend(g, hh, bg, bg_every)
            if bg is not None:
                for _ in bg:
                    pass
            state["post"] = (ps, g)
            if g == NB - 1:
                run_fin2()


def rope_tables_np(S):
    pos = np.arange(S, dtype=np.float32)
    inv_freq = (10000.0 ** (-np.arange(0, 64, 2, dtype=np.float32) / 64)).astype(np.float32)
    ang = pos[None, :] * inv_freq[:, None]
    c = np.cos(ang).astype(np.float32)
    s = np.sin(ang).astype(np.float32)
    rc = np.concatenate([c, c, c, c], axis=0)
    rs = np.concatenate([-s, s, -s, s], axis=0)
    return np.ascontiguousarray(rc), np.ascontiguousarray(rs)


def build_prog_attention(S=SEQ, HPC=4):
    nc = bass.Bass("TRN2", target_bir_lowering=False)
    io = {}
    io["xT"] = nc.dram_tensor("xT", [D, S], F32, kind="ExternalInput").ap()
    for n in ("wq", "wk", "wv"):
        io[n] = nc.dram_tensor(n, [D, HPC * 128], F32, kind="ExternalInput").ap()
    io["rc"] = nc.dram_tensor("rc", [128, S], F32, kind="ExternalInput").ap()
    io["rs"] = nc.dram_tensor("rs", [128, S], F32, kind="ExternalInput").ap()
    io["lam"] = nc.dram_tensor("lam", [128, 256], F32, kind="ExternalInput").ap()
    io["subg"] = nc.dram_tensor("subg", [128, 1], F32, kind="ExternalInput").ap()
    io["aoT"] = nc.dram_tensor("aoT", [HPC * 128, S], BF16, kind="ExternalOutput").ap()
    with ExitStack() as es:
        p = Prog(nc)
        build_attention(nc, es, p, S, io, HPC)
        p.emit(es)
    return nc


def attention_inputs(x_b, w_qkv, lq1, lk1, lq2, lk2, subg, h0, HPC, S):
    rc, rs = rope_tables_np(S)
    lam = np.concatenate([lq1, lk1, lq2, lk2]).astype(np.float32)[None, :].repeat(128, axis=0)
    return {
        "xT": np.ascontiguousarray(x_b[:S].T),
        "wq": np.ascontiguousarray(w_qkv[:, h0 * 128:(h0 + HPC) * 128]),
        "wk": np.ascontiguousarray(w_qkv[:, D + h0 * 128:D + (h0 + HPC) * 128]),
        "wv": np.ascontiguousarray(w_qkv[:, 2 * D + h0 * 128:2 * D + (h0 + HPC) * 128]),
        "rc": rc, "rs": rs, "lam": np.ascontiguousarray(lam),
        "subg": np.ascontiguousarray(subg.reshape(128, 1).astype(np.float32)),
    }


VC_LN = 0
VC_BPW1 = 64
VC_WDW = 80
VC_BDW = 328
VC_CLNG = 336
VC_CLNB = 344
VC_BPW2 = 352
NVEC = 360
RING_ELEMS = 2944
NSLOT = 7


def pack_vecs(inp):
    cols = lambda v: np.asarray(v, np.float32).reshape(-1, 128).T
    out = np.zeros((128, NVEC), np.float32)
    for i in range(2):
        for which in range(2):
            base = VC_LN + ((i * 2 + which) * 2) * 8
            out[:, base:base + 8] = cols(inp["ln_g"][i, which])
            out[:, base + 8:base + 16] = cols(inp["ln_b"][i, which])
    out[:, VC_BPW1:VC_BPW1 + 16] = cols(inp["conv_b_pw1"][0])
    wdw = np.asarray(inp["conv_w_dw"][0], np.float32).reshape(CW, 8, 128).transpose(2, 0, 1).reshape(128, CW * 8)
    out[:, VC_WDW:VC_WDW + CW * 8] = wdw
    out[:, VC_BDW:VC_BDW + 8] = cols(inp["conv_b_dw"][0])
    out[:, VC_CLNG:VC_CLNG + 8] = cols(inp["conv_ln_g"][0])
    out[:, VC_CLNB:VC_CLNB + 8] = cols(inp["conv_ln_b"][0])
    out[:, VC_BPW2:VC_BPW2 + 8] = cols(inp["conv_b_pw2"][0])
    return out


def declare_local_io(nc, NT, with_a=True):
    io = {}
    if with_a:
        io["aT"] = nc.dram_tensor("aT", [D, NT], BF16, kind="ExternalInput").ap()
    io["xTl"] = nc.dram_tensor("xTl", [D, NT], F32, kind="ExternalInput").ap()
    for n, shp in (("wo", [D, D]), ("wg0", [D, DFF]), ("wu0", [D, DFF]), ("wd0", [DFF, D]), ("pw1", [D, 2 * D]),
                   ("pw2", [D, D]), ("wg1", [D, DFF]), ("wu1", [D, DFF]), ("wd1", [DFF, D])):
        io[n] = nc.dram_tensor(n, shp, F32, kind="ExternalInput").ap()
    io["vecs"] = nc.dram_tensor("vecs", [128, NVEC], F32, kind="ExternalInput").ap()
    io["ident"] = nc.dram_tensor("ident", [128, 128], F32, kind="ExternalInput").ap()
    io["hscale"] = nc.dram_tensor("hscale", [128, 1], F32, kind="ExternalInput").ap()
    io["oT"] = nc.dram_tensor("oT", [D, NT - 128], F32, kind="ExternalOutput").ap()
    return io


def make_prep(nc, es, p, io, prepq="act"):
    T = lambda name, shape, dt, space="sbuf", nbufs=1: Tile(p, es, name, shape, dt, space, nbufs)
    def scratch(name, nj, per):
        t = nc.dram_tensor(name, [nj, 128, per], BF16, kind="Internal").ap()
        return t, [Buf(f"{name}.{j}") for j in range(nj)]
    s_wo, b_wo = scratch("s_wo", 8, 1024)
    s_gu = [scratch("s_gu0", NF, 2048), scratch("s_gu1", NF, 2048)]
    s_wd = [scratch("s_wd0", 8, NF * 128), scratch("s_wd1", 8, NF * 128)]
    s_pw1, b_pw1 = scratch("s_pw1", 8, 2048)
    s_pw2, b_pw2 = scratch("s_pw2", 8, 1024)

    s_dg, b_dg = scratch("s_dg", 16, 2048)
    stg = T("b_stg", [128, 2, 1408], F32, nbufs=2)
    stb = T("b_stb", [128, 2, 1408], BF16, nbufs=2)
    ident = T("b_ident", [128, 128], F32)
    pvec = T("b_pvec", [128, CW * 8], F32)
    dgs = T("b_dgs", [128, 1, 2048], BF16, nbufs=1)
    p.dma("sp", ident[:], io["ident"][:, :], writes=[ident.b[0]], key="b_ident")
    p.dma("sp", pvec[:], io["vecs"][:, VC_WDW:VC_WDW + CW * 8], writes=[pvec.b[0]], key="b_pvec")
    dst_ = {"i": 0}

    def prep_diag(j, half):
        sl = 0
        k0 = KD + 16 * half
        nk = 16 if half == 0 else CW - KD - 16
        for t in range(nk):
            k = k0 + t
            (lambda t, k: p.op("dve", lambda e: e.tensor_scalar(out=dgs[:, sl, t * 128:(t + 1) * 128], in0=ident[:],
                                                                scalar1=pvec[:, k * 8 + j:k * 8 + j + 1], scalar2=None, op0=ALU.mult),
                               reads=[ident.b[0], pvec.b[0]], writes=[dgs.b[sl]]))(t, k)
        p.dma(prepq, s_dg[2 * j + half, :, :nk * 128], dgs[:, sl, :nk * 128], reads=[dgs.b[sl]], writes=[b_dg[2 * j + half]],
              key=("b_dgs", sl))
    pst = {"i": 0}

    pend = []

    def flush():
        for sl, (src_ap, nel, dst_ap, dst_buf) in enumerate(pend):
            kc = nel // 128
            p.dma(prepq, stg[:, sl, :nel].rearrange("p (k c) -> p k c", k=kc), src_ap, writes=[stg.b[sl]], key=("b_stg", sl))
        for sl, (src_ap, nel, dst_ap, dst_buf) in enumerate(pend):
            (lambda sl, nel: p.op("pool", lambda e: e.tensor_copy(out=stb[:, sl, :nel], in_=stg[:, sl, :nel]),
                                  reads=[stg.b[sl]], writes=[stb.b[sl]]))(sl, nel)
            p.dma(prepq, dst_ap, stb[:, sl, :nel], reads=[stb.b[sl]], writes=[dst_buf], key=("b_stb", sl))
        pend.clear()

    def prep_unit(src_ap, nel, dst_ap, dst_buf):
        pend.append((src_ap, nel, dst_ap, dst_buf))
        if len(pend) == 2:
            flush()

    def prep_k1024(w_ap, col0, dst_scr, j, off, dst_buf):
        src = w_ap.rearrange("(kc p) n -> p kc n", p=128)[:, :, col0:col0 + 128]
        prep_unit(src, 1024, dst_scr[j, :, off:off + 1024], dst_buf)

    def prep_wd(w_ap, dst_scr, j, dst_buf):
        v = w_ap.rearrange("(kc p) n -> p kc n", p=128)
        for half in range(2):
            prep_unit(v[:, half * 11:(half + 1) * 11, j * 128:(j + 1) * 128], 1408,
                      dst_scr[j, :, half * 1408:(half + 1) * 1408], dst_buf)

    def prep_gen():
        for j in range(8):
            prep_k1024(io["wo"], j * 128, s_wo, j, 0, b_wo[j])
            yield
        for j in range(8):
            for half in range(2):
                prep_diag(j, half)
                yield
        for i in range(2):
            if i == 1:
                for j in range(8):
                    prep_k1024(io["pw1"], j * 128, s_pw1, j, 0, b_pw1[j])
                    prep_k1024(io["pw1"], D + j * 128, s_pw1, j, 1024, b_pw1[j])
                    yield
                for j in range(8):
                    prep_k1024(io["pw2"], j * 128, s_pw2, j, 0, b_pw2[j])
                    yield
            for j in range(NF):
                prep_k1024(io[f"wg{i}"], j * 128, s_gu[i][0], j, 0, s_gu[i][1][j])
                prep_k1024(io[f"wu{i}"], j * 128, s_gu[i][0], j, 1024, s_gu[i][1][j])
                yield
            for j in range(8):
                prep_wd(io[f"wd{i}"], s_wd[i][0], j, s_wd[i][1][j])
                yield
        flush()

    scr = dict(s_wo=s_wo, b_wo=b_wo, s_gu=s_gu, s_wd=s_wd, s_pw1=s_pw1, b_pw1=b_pw1, s_pw2=s_pw2, b_pw2=b_pw2,
               s_dg=s_dg, b_dg=b_dg)
    return scr, prep_gen()


def build_local(nc, es, p, io, NT, scr, load_a=None):
    T = lambda name, shape, dt, space="sbuf", nbufs=1: Tile(p, es, name, shape, dt, space, nbufs)
    s_wo, b_wo, s_gu, s_wd = scr["s_wo"], scr["b_wo"], scr["s_gu"], scr["s_wd"]
    s_pw1, b_pw1, s_pw2, b_pw2 = scr["s_pw1"], scr["b_pw1"], scr["s_pw2"], scr["b_pw2"]
    s_dg, b_dg = scr["s_dg"], scr["b_dg"]
    vec = T("b_vec", [128, NVEC], F32)
    hsc = T("b_hsc", [128, 1], F32)
    onesm = T("b_onesm", [128, 128], BF16)
    ring = T("b_ring", [128, NSLOT, RING_ELEMS], BF16, nbufs=NSLOT)
    X0 = T("b_x0", [128, 1, 8, 512], F32, nbufs=8)
    Abf = T("b_abf", [128, 2, 8, 512], BF16, nbufs=8)
    XA = T("b_xa", [128, 8, 512], F32, nbufs=8)
    XB = T("b_xb", [128, 8, 512], F32, nbufs=8)
    XbfA = T("b_xbfa", [128, 8, 512], BF16, nbufs=8)
    XbfB = T("b_xbfb", [128, 8, 512], BF16, nbufs=8)
    XC = T("b_xc", [128, 8, 512], F32, nbufs=8)
    XbfC = T("b_xbfc", [128, 8, 512], BF16, nbufs=8)
    zb = T("b_zb", [128, 3, 512], BF16, nbufs=3)
    zsq = T("b_zsq", [128, 3, 512], BF16, nbufs=3)
    hT = T("b_hT", [128, NF, 512], BF16, nbufs=NF)
    hbuf = T("b_hbuf", [128, 8, 544], BF16, nbufs=8)
    hb_tail = [Buf(f"hb_tail{j}") for j in range(8)]
    tmpa = T("b_tmpa", [128, 2, 512], F32, nbufs=2)
    tn1 = T("b_tn1", [128, 2, 512], F32, nbufs=2)
    tn2 = T("b_tn2", [128, 2, 512], F32, nbufs=2)
    st_msq = T("b_msq", [128, 512], F32)
    st_mean = T("b_mean", [128, 512], F32)
    st_var = T("b_var", [128, 512], F32)
    st_rstd = T("b_rstd", [128, 512], F32)
    pg = T("b_pg", [128, 4, 512], F32, "psum", nbufs=4)
    pp = T("b_pp", [128, 2, 512], F32, "psum", nbufs=2)
    pst_ = T("b_pst", [128, 2, 512], F32, "psum", nbufs=2)

    p.op("pool", lambda e: e.memset(onesm[:], 1.0 / 1024.0), writes=[onesm.b[0]])
    p.dma("sp", vec[:], io["vecs"][:, :], writes=[vec.b[0]], key="b_vec")
    p.dma("sp", hsc[:], io["hscale"][:, :], writes=[hsc.b[0]], key="b_hsc")
    vcol = lambda c: vec[:, c:c + 1]

    rs = {"i": 0, "z": 0, "t": 0, "n": 0, "pp": 0, "pg": 0}

    def fetch(src_ap, nel, src_buf):
        slot = rs["i"] % NSLOT
        rs["i"] += 1
        p.dma("sp", ring[:, slot, :nel], src_ap, reads=[src_buf], writes=[ring.b[slot]], key=("b_ring", slot))
        return slot

    def mm(out, lhsT, rhs, start, stop, reads, writes):
        p.op("pe", lambda e: e.matmul(out, lhsT=lhsT, rhs=rhs, start=start, stop=stop), reads=reads, writes=writes)

    xT3 = io["xTl"].rearrange("(kc p) t -> p kc t", p=128)
    aT3 = io["aT"].rearrange("(h p) t -> p h t", p=128) if "aT" in io else None
    oT3 = io["oT"].rearrange("(kc p) t -> p kc t", p=128)

    def load_group(gi, tok0, W):
        sl = 0
        p.dma("sp", X0[:, sl, :, :W], xT3[:, :, tok0:tok0 + W], writes=[X0.b[sl * 8 + j] for j in range(8)], key=("b_x0", sl))
        if load_a is not None:
            load_a(gi, tok0, W, Abf, hsc)
        else:
            p.dma("sp", Abf[:, sl, :, :W], aT3[:, :, tok0:tok0 + W], writes=Abf.b[0:4], key=("b_abf", sl))

    def ln_core(W, dstx, src_is_dst_bufs, gcol, bcol, out_fn):
        p.op("act", lambda e: e.activation(out=st_msq[:, :W], in_=pst_[:, 0, :W], func=AF.Square),
             reads=[pst_.b[0]], writes=[st_msq.b[0]])
        p.op("act", lambda e: e.activation(out=st_mean[:, :W], in_=pst_[:, 0, :W], func=AF.Identity),
             reads=[pst_.b[0]], writes=[st_mean.b[0]])
        p.op("dve", lambda e: e.tensor_tensor(out=st_var[:, :W], in0=pst_[:, 1, :W], in1=st_msq[:, :W], op=ALU.subtract),
             reads=[pst_.b[1], st_msq.b[0]], writes=[st_var.b[0]])
        p.op("act", lambda e: e.activation(out=st_var[:, :W], in_=st_var[:, :W], func=AF.Ln, bias=EPS),
             reads=[st_var.b[0]], writes=[st_var.b[0]])
        p.op("act", lambda e: e.activation(out=st_rstd[:, :W], in_=st_var[:, :W], func=AF.Exp, scale=-0.5),
             reads=[st_var.b[0]], writes=[st_rstd.b[0]])
        for j in range(8):
            s1 = rs["n"] % 2
            rs["n"] += 1
            (lambda j, s1: (
                p.op("pool" if j % 2 == 0 else "dve",
                     lambda e: e.tensor_tensor(out=tn1[:, s1, :W], in0=dstx[:, j, :W], in1=st_mean[:, :W], op=ALU.subtract),
                     reads=[dstx.b[j], st_mean.b[0]], writes=[tn1.b[s1]]),
                p.op("dve", lambda e: e.tensor_tensor(out=tn2[:, s1, :W], in0=tn1[:, s1, :W], in1=st_rstd[:, :W], op=ALU.mult),
                     reads=[tn1.b[s1], st_rstd.b[0]], writes=[tn2.b[s1]]),
                out_fn(j, tn2[:, s1, :W], tn2.b[s1])))(j, s1)

    def stats_mm(W, j, s):
        mm(pst_[:, 0, :W], onesm[:], zb[:, s, :W], j == 0, j == 7, [onesm.b[0], zb.b[s]], [pst_.b[0]])
        mm(pst_[:, 1, :W], onesm[:], zsq[:, s, :W], j == 0, j == 7, [onesm.b[0], zsq.b[s]], [pst_.b[1]])

    def z_stats(W, dstx, j):
        s = rs["z"] % 3
        rs["z"] += 1
        p.op("dve", lambda e: e.tensor_copy(out=zb[:, s, :W], in_=dstx[:, j, :W]), reads=[dstx.b[j]], writes=[zb.b[s]])
        p.op("act", lambda e: e.activation(out=zsq[:, s, :W], in_=dstx[:, j, :W], func=AF.Square), reads=[dstx.b[j]], writes=[zsq.b[s]])
        return s

    def resid_ln(W, srcx_ap_fn, srcx_buf_fn, dstx, dstbf, lnidx, proj_chunk, bias_col=None, store=None):
        gcol = VC_LN + lnidx * 16
        bcol = gcol + 8
        pend = None
        for j in range(8):
            pa, pb = proj_chunk(j)
            if bias_col is not None:
                t = rs["t"] % 2
                rs["t"] += 1
                (lambda j, t, pa, pb: p.op("act", lambda e: e.activation(out=tmpa[:, t, :W], in_=pa, func=AF.Identity,
                                                                         bias=vcol(bias_col + j)),
                                           reads=[pb, vec.b[0]], writes=[tmpa.b[t]]))(j, t, pa, pb)
                pa, pb = tmpa[:, t, :W], tmpa.b[t]
            (lambda j, pa, pb: p.op("dve", lambda e: e.scalar_tensor_tensor(
                out=dstx[:, j, :W], in0=srcx_ap_fn(j), scalar=ALPHA, in1=pa, op0=ALU.mult, op1=ALU.add),
                reads=[srcx_buf_fn(j), pb], writes=[dstx.b[j]]))(j, pa, pb)
            s = z_stats(W, dstx, j)
            if pend is not None:
                stats_mm(W, *pend)
            pend = (j, s)
        stats_mm(W, *pend)

        def out_fn(j, t2, t2b):
            if dstbf is not None:
                p.op("act", lambda e: e.activation(out=dstbf[:, j, :W], in_=t2, func=AF.Identity, scale=vcol(gcol + j), bias=vcol(bcol + j)),
                     reads=[t2b, vec.b[0]], writes=[dstbf.b[j]])
            p.op("act", lambda e: e.activation(out=dstx[:, j, :W], in_=t2, func=AF.Identity, scale=vcol(gcol + j), bias=vcol(bcol + j)),
                 reads=[t2b, vec.b[0]], writes=[dstx.b[j]])
            if store is not None:
                store(j)
        ln_core(W, dstx, None, gcol, bcol, out_fn)

    def next_pp():
        b = rs["pp"] % 2
        rs["pp"] += 1
        return b

    def ffn(W, i, xbf, srcx, dstx, dstbf, lnidx, store=None, mid_hook=None):
        s_g, b_g = s_gu[i]
        s_d, b_d = s_wd[i]

        def ffn_epi(j, pb0):
            t_ = rs["t"] % 2
            rs["t"] += 1
            p.op("act", lambda e: e.activation(out=tmpa[:, t_, :W], in_=pg[:, pb0, :W], func=AF.Silu),
                 reads=[pg.b[pb0]], writes=[tmpa.b[t_]])
            p.op("dve", lambda e: e.tensor_tensor(out=hT[:, j, :W], in0=tmpa[:, t_, :W], in1=pg[:, pb0 + 1, :W], op=ALU.mult),
                 reads=[tmpa.b[t_], pg.b[pb0 + 1]], writes=[hT.b[j]])

        j = 0
        while j < NF:
            nj = 2 if j == 0 else 1
            slots, pbs = [], []
            for jj in range(nj):
                slots.append(fetch(s_g[j + jj, :, :], 2048, b_g[j + jj]))
                pbs.append((rs["pg"] % 2) * 2)
                rs["pg"] += 1
            if nj == 2:
                for kc in range(8):
                    for jj in range(2):
                        for t in range(2):
                            off = t * 1024 + kc * 128
                            mm(pg[:, pbs[jj] + t, :W], ring[:, slots[jj], off:off + 128], xbf[:, kc, :W], kc == 0, kc == 7,
                               [ring.b[slots[jj]], xbf.b[kc]], [pg.b[pbs[jj] + t]])
            else:
                for t in range(2):
                    for kc in range(8):
                        off = t * 1024 + kc * 128
                        mm(pg[:, pbs[0] + t, :W], ring[:, slots[0], off:off + 128], xbf[:, kc, :W], kc == 0, kc == 7,
                           [ring.b[slots[0]], xbf.b[kc]], [pg.b[pbs[0] + t]])
            for jj in range(nj):
                ffn_epi(j + jj, pbs[jj])
            j += nj

        if mid_hook is not None:
            mid_hook()

        def proj_chunk(j):
            slot = fetch(s_d[j, :, :], NF * 128, b_d[j])
            b = next_pp()
            for kc in range(NF):
                mm(pp[:, b, :W], ring[:, slot, kc * 128:(kc + 1) * 128], hT[:, kc, :W], kc == 0, kc == NF - 1,
                   [ring.b[slot], hT.b[kc]], [pp.b[b]])
            return pp[:, b, :W], pp.b[b]
        resid_ln(W, lambda j: srcx[:, j, :W], lambda j: srcx.b[j], dstx, dstbf, lnidx, proj_chunk, store=store)

    def stage_wo(W):
        sl = 0

        def proj_wo(j):
            slot = fetch(s_wo[j, :, :], 1024, b_wo[j])
            b = next_pp()
            for h in range(8):
                mm(pp[:, b, :W], ring[:, slot, h * 128:(h + 1) * 128], Abf[:, sl, h, :W], h == 0, h == 7,
                   [ring.b[slot]] + Abf.b[0:4], [pp.b[b]])
            return pp[:, b, :W], pp.b[b]
        resid_ln(W, lambda j: X0[:, sl, j, :W], lambda j: X0.b[sl * 8 + j], XC, XbfC, 0, proj_wo)

    def group(gi, tok0, W, halo, after_ffn0_hidden=None, next_wo=None):
        ffn(W, 0, XbfC, XC, XB, XbfB, 1, mid_hook=after_ffn0_hidden)
        def glu_epi(j, pb0):
            t_ = rs["t"] % 2
            rs["t"] += 1
            p.op("act", lambda e: e.activation(out=tmpa[:, t_, :W], in_=pg[:, pb0 + 1, :W], func=AF.Sigmoid,
                                               bias=vcol(VC_BPW1 + 8 + j)),
                 reads=[pg.b[pb0 + 1], vec.b[0]], writes=[tmpa.b[t_]])
            p.op("dve", lambda e: e.scalar_tensor_tensor(out=hbuf[:, j, 32:32 + W], in0=pg[:, pb0, :W], scalar=vcol(VC_BPW1 + j),
                                                         in1=tmpa[:, t_, :W], op0=ALU.add, op1=ALU.mult),
                 reads=[pg.b[pb0], tmpa.b[t_], vec.b[0]], writes=[hbuf.b[j]])

        j = 0
        while j < 8:
            nj = 2 if j == 0 else 1
            slots, pbs = [], []
            for jj in range(nj):
                slots.append(fetch(s_pw1[j + jj, :, :], 2048, b_pw1[j + jj]))
                pbs.append((rs["pg"] % 2) * 2)
                rs["pg"] += 1
            order = [(kc, jj, t) for kc in range(8) for jj in range(nj) for t in range(2)] if nj == 2 else \
                    [(kc, 0, t) for t in range(2) for kc in range(8)]
            for (kc, jj, t) in order:
                off = t * 1024 + kc * 128
                mm(pg[:, pbs[jj] + t, :W], ring[:, slots[jj], off:off + 128], XbfB[:, kc, :W], kc == 0, kc == 7,
                   [ring.b[slots[jj]], XbfB.b[kc]], [pg.b[pbs[jj] + t]])
            for jj in range(nj):
                glu_epi(j + jj, pbs[jj])
            j += nj
        if halo:
            for j in range(8):
                (lambda j: p.op("dve", lambda e: e.tensor_scalar(out=hbuf[:, j, 32:32 + W], in0=hbuf[:, j, 32:32 + W],
                                                                  scalar1=hsc[:, 0:1], scalar2=None, op0=ALU.mult),
                                reads=[hsc.b[0]], writes=[hbuf.b[j]]))(j)
        else:
            n1 = CW - KD - 16
            for jp in range(0, 8, 2):
                for k in range(KD):
                    for j in (jp, jp + 1):
                        wc = vcol(VC_WDW + k * 8 + j)
                        src = hbuf[:, j, 2 + k:2 + k + W]
                        if k == 0:
                            (lambda j, wc, src: p.op("dve", lambda e: e.tensor_scalar(
                                out=XA[:, j, :W], in0=src, scalar1=wc, scalar2=None, op0=ALU.mult),
                                reads=[hbuf.b[j], hb_tail[j], vec.b[0]], writes=[XA.b[j]]))(j, wc, src)
                        else:
                            (lambda j, wc, src: p.op("dve", lambda e: e.scalar_tensor_tensor(
                                out=XA[:, j, :W], in0=src, scalar=wc, in1=XA[:, j, :W], op0=ALU.mult, op1=ALU.add),
                                reads=[hbuf.b[j], hb_tail[j], vec.b[0]], writes=[XA.b[j]]))(j, wc, src)
                for j in (jp, jp + 1):
                    slots = [fetch(s_dg[2 * j, :, :], 2048, b_dg[2 * j]),
                             fetch(s_dg[2 * j + 1, :, :n1 * 128], n1 * 128, b_dg[2 * j + 1])]
                    b = next_pp()
                    for k in range(KD, CW):
                        sl_, t = slots[(k - KD) // 16], (k - KD) % 16
                        mm(pp[:, b, :W], ring[:, sl_, t * 128:(t + 1) * 128], hbuf[:, j, 2 + k:2 + k + W], k == KD, k == CW - 1,
                           [ring.b[sl_], hbuf.b[j], hb_tail[j]], [pp.b[b]])
                    (lambda j, b: p.op("dve", lambda e: e.scalar_tensor_tensor(
                        out=XA[:, j, :W], in0=pp[:, b, :W], scalar=vcol(VC_BDW + j), in1=XA[:, j, :W], op0=ALU.add, op1=ALU.add),
                        reads=[pp.b[b], vec.b[0]], writes=[XA.b[j]]))(j, b)
        for j in range(8):
            (lambda j: p.op("pool", lambda e: e.tensor_copy(out=hbuf[:, j, 0:32], in_=hbuf[:, j, W:W + 32]),
                            reads=[hbuf.b[j]], writes=[hb_tail[j]]))(j)
        if halo:
            if next_wo is not None:
                next_wo()
            return
        pend = None
        for j in range(8):
            s = z_stats(W, XA, j)
            if pend is not None:
                stats_mm(W, *pend)
            pend = (j, s)
        stats_mm(W, *pend)

        def out_silu(j, t2, t2b):
            p.op("act", lambda e: e.activation(out=XbfA[:, j, :W], in_=t2, func=AF.Silu, scale=vcol(VC_CLNG + j), bias=vcol(VC_CLNB + j)),
                 reads=[t2b, vec.b[0]], writes=[XbfA.b[j]])
        ln_core(W, XA, None, 0, 0, out_silu)
        def proj_pw2(j):
            slot = fetch(s_pw2[j, :, :], 1024, b_pw2[j])
            b = next_pp()
            for kc in range(8):
                mm(pp[:, b, :W], ring[:, slot, kc * 128:(kc + 1) * 128], XbfA[:, kc, :W], kc == 0, kc == 7,
                   [ring.b[slot], XbfA.b[kc]], [pp.b[b]])
            return pp[:, b, :W], pp.b[b]
        resid_ln(W, lambda j: XB[:, j, :W], lambda j: XB.b[j], XA, XbfB, 2, proj_pw2, bias_col=VC_BPW2)
        o0 = tok0 - 128

        def store(j):
            p.dma("act", oT3[:, j, o0:o0 + W], XB[:, j, :W], reads=[XB.b[j]], key=("b_out", j))
        ffn(W, 1, XbfB, XA, XB, None, 3, store=store, mid_hook=next_wo)

    groups = [(96, 32, True)] + [(128 + 512 * i, 512, False) for i in range((NT - 128) // 512)]
    load_group(0, *groups[0][:2])
    stage_wo(groups[0][1])
    for gi, (tok0, W, halo) in enumerate(groups):
        hook = None
        nwo = None
        if gi + 1 < len(groups):
            hook = (lambda gi: lambda: load_group(gi + 1, *groups[gi + 1][:2]))(gi)
            nwo = (lambda gi: lambda: stage_wo(groups[gi + 1][1]))(gi)
        group(gi, tok0, W, halo, hook, nwo)


def build_prog_local(NT):
    nc = bass.Bass("TRN2", target_bir_lowering=False)
    io = declare_local_io(nc, NT)
    with ExitStack() as es:
        p = Prog(nc)
        esP = es.enter_context(ExitStack())
        scr, gen = make_prep(nc, esP, p, io)
        for _ in gen:
            pass
        p.barrier()
        esP.close()
        build_local(nc, es, p, io, NT, scr)
        p.emit(es)
    return nc


NT_LOCAL = 128 + SEQ // 2
_CACHE = {}


def _get(name, fn):
    if name not in _CACHE:
        _CACHE[name] = fn()
    return _CACHE[name]


CH_T = [(0, 2176), (2176, 2048)]
PAIRS = [[0, 1], [2, 3], [4, 5], [6, 7]]
PREP_PER_BLOCK = 4


def build_prog_fused():
    S = SEQ
    nc = bass.Bass("TRN2", target_bir_lowering=False)
    ioA = {}
    ioA["xT"] = nc.dram_tensor("xT", [D, S], F32, kind="ExternalInput").ap()
    for n in ("wq", "wk", "wv"):
        ioA[n] = nc.dram_tensor(n, [D, 512], F32, kind="ExternalInput").ap()
    ioA["rc"] = nc.dram_tensor("rc", [128, S], F32, kind="ExternalInput").ap()
    ioA["rs"] = nc.dram_tensor("rs", [128, S], F32, kind="ExternalInput").ap()
    ioA["lam"] = nc.dram_tensor("lam", [128, 256], F32, kind="ExternalInput").ap()
    ioA["subg"] = nc.dram_tensor("subg", [128, 1], F32, kind="ExternalInput").ap()
    ioB = declare_local_io(nc, NT_LOCAL, with_a=False)
    snd, rcv, sndb, rcvb = {}, {}, {}, {}
    for ps in range(2):
        for h in range(2):
            for c in range(2):
                k = (ps, h, c)
                snd[k] = nc.dram_tensor(f"snd_{ps}{h}{c}", [256, CH_T[c][1]], BF16, kind="Internal").ap()
                rcv[k] = nc.dram_tensor(f"rcv_{ps}{h}{c}", [512, CH_T[c][1]], BF16, kind="Internal").ap()
                sndb[k] = Buf(f"snd{k}")
                rcvb[k] = Buf(f"rcv{k}")
    with ExitStack() as es:
        p = Prog(nc)
        zt = Tile(p, es, "f_zero", [128, 2, 128], BF16)
        sel = Tile(p, es, "f_sel", [128, 2], F32)
        esA = es.enter_context(ExitStack())
        scr, gen = make_prep(nc, esA, p, ioB, prepq="pool")
        p.op("pool", lambda e: e.memset(zt[:], 0.0), writes=[zt.b[0]])
        for ps in range(2):
            k = (ps, 0, 0)
            p.dma("sp", snd[k].rearrange("(hh p) t -> p hh t", p=128)[:, :, 0:128], zt[:], reads=[zt.b[0]],
                  writes=[sndb[k]], key=("f_z", ps))

        def out_fn(ps, hh, g, fout, fb):
            h, kk = g // 8, g % 8
            c = 0 if kk < 4 else 1
            lt0 = 128 + 512 * kk - CH_T[c][0]
            k = (ps, h, c)
            p.dma("sp", snd[k][hh * 128:(hh + 1) * 128, lt0:lt0 + 512], fout[:, fb, :], reads=[fout.b[fb]],
                  writes=[sndb[k]], key=("a_out", fb))
            if g == 7:
                k2 = (ps, 1, 0)
                p.dma("sp", snd[k2][hh * 128:(hh + 1) * 128, 0:128], fout[:, fb, 384:512], reads=[fout.b[fb]],
                      writes=[sndb[k2]], key=("a_out2", fb))

        def block_hook(ps, g):
            for _ in range(int(math.ceil((g + 1) * 0.4))):
                next(gen, None)
            if g % 4 == 3:
                k = (ps, g // 8, (g % 8) // 4)
                p.collective((lambda k: lambda e: e.collective_compute(
                    "AllGather", ALU.bypass, replica_groups=PAIRS, ins=[snd[k][:, :]], outs=[rcv[k][:, :]]))(k),
                    reads=[sndb[k]], writes=[rcvb[k]], key=("cc",) + k)

        build_attention(nc, esA, p, S, ioA, 4, out_fn=out_fn, block_hook=block_hook)
        for _ in gen:
            pass
        p.barrier()
        esA.close()
        st = {"sel": False}

        def load_a(gi, tok0, W, Abf, hsc):
            if not st["sel"]:
                st["sel"] = True
                p.op("pool", lambda e: e.tensor_copy(out=sel[:, 1:2], in_=hsc[:, 0:1]), reads=[hsc.b[0]], writes=[sel.b[0]])
                p.op("pool", lambda e: e.tensor_scalar(out=sel[:, 0:1], in0=hsc[:, 0:1], scalar1=-1.0, scalar2=1.0,
                                                       op0=ALU.mult, op1=ALU.add), reads=[hsc.b[0]], writes=[sel.b[0]])
            c = 0 if tok0 < CH_T[1][0] else 1
            off = tok0 - CH_T[c][0]
            for h in range(2):
                for ps in range(2):
                    k = (ps, h, c)
                    v = rcv[k].rearrange("(r hh p) t -> p r hh t", r=2, hh=2, p=128)
                    for r in range(2):
                        hd = r * 4 + ps * 2
                        p.dma("sp", Abf[:, h, hd:hd + 2, :W], v[:, r, :, off:off + W], reads=[rcvb[k]],
                              writes=[Abf.b[h * 4 + ps * 2 + r]], key=("b_abf", h, ps, r))
            p.op("act", lambda e: e.activation(out=Abf[:, 0, :, :W], in_=Abf[:, 0, :, :W], func=AF.Identity, scale=sel[:, 0:1]),
                 reads=[sel.b[0]], writes=Abf.b[0:4])
            p.op("act", lambda e: e.activation(out=Abf[:, 1, :, :W], in_=Abf[:, 1, :, :W], func=AF.Identity, scale=sel[:, 1:2]),
                 reads=[sel.b[0]], writes=Abf.b[4:8])
            p.op("pool", lambda e: e.tensor_tensor(out=Abf[:, 0, :, :W], in0=Abf[:, 0, :, :W], in1=Abf[:, 1, :, :W], op=ALU.add),
                 reads=Abf.b[4:8], writes=Abf.b[0:4])

        with ExitStack() as esB:
            build_local(nc, esB, p, ioB, NT_LOCAL, scr, load_a=load_a)
            p.emit(es)
    return nc


def kernel(x, attn_w_qkv, attn_w_o, attn_lambda_q1, attn_lambda_k1, attn_lambda_q2, attn_lambda_k2, attn_subln_g,
                 conv_w_pw1, conv_b_pw1, conv_w_dw, conv_b_dw, conv_ln_g, conv_ln_b, conv_w_pw2, conv_b_pw2,
                 ffn_w_gate, ffn_w_up, ffn_w_down, ln_g, ln_b):
    f = lambda a: np.asarray(a, dtype=np.float32)
    x = f(x)
    inp = dict(ln_g=f(ln_g), ln_b=f(ln_b), conv_b_pw1=f(conv_b_pw1), conv_w_dw=f(conv_w_dw), conv_b_dw=f(conv_b_dw),
               conv_ln_g=f(conv_ln_g), conv_ln_b=f(conv_ln_b), conv_b_pw2=f(conv_b_pw2))
    B = x.shape[0]
    H = SEQ // 2
    wqkv = f(attn_w_qkv)[0]
    nc = _get("F", build_prog_fused)
    xTs = [np.ascontiguousarray(x[b].T) for b in range(B)]
    shared = {"wo": f(attn_w_o)[0], "wg0": f(ffn_w_gate)[0], "wu0": f(ffn_w_up)[0], "wd0": f(ffn_w_down)[0],
              "pw1": f(conv_w_pw1)[0], "pw2": f(conv_w_pw2)[0], "wg1": f(ffn_w_gate)[1], "wu1": f(ffn_w_up)[1],
              "wd1": f(ffn_w_down)[1], "vecs": pack_vecs(inp), "ident": np.eye(128, dtype=np.float32)}
    shared = {k: np.ascontiguousarray(v) for k, v in shared.items()}
    in_maps = []
    for c in range(8):
        b, half = c // 2, c % 2
        m = attention_inputs(x[b], wqkv, f(attn_lambda_q1)[0], f(attn_lambda_k1)[0], f(attn_lambda_q2)[0],
                             f(attn_lambda_k2)[0], f(attn_subln_g)[0], 4 * half, 4, SEQ)
        m["xT"] = xTs[b]
        m.update(shared)
        if half == 0:
            xTl = np.concatenate([np.zeros((D, 128), np.float32), xTs[b][:, :H]], axis=1)
        else:
            xTl = xTs[b][:, H - 128:]
        m["xTl"] = np.ascontiguousarray(xTl)
        m["hscale"] = np.full((128, 1), float(half), np.float32)
        in_maps.append(m)
    res = run_bass_kernel_spmd(nc, in_maps, core_ids=list(range(8)))
    out = np.empty((B, SEQ, D), np.float32)
    for c in range(8):
        b, half = c // 2, c % 2
        out[b, half * H:(half + 1) * H, :] = res.results[c]["oT"].T
    return out
```
